# Optimizing a Trainium2 kernel written in Bass

```python
import math
import jax, jax.numpy as jnp
from jax import lax
import numpy as np

D_MODEL = 1024
BATCH = 8
SEQ = 4096
DEPTH = 2

CHUNK = 64
Q_BLOCK = 128
PLE_DIM = 256
D_FF = 2816
EPS = 1e-6
SSD_HEADS = 16
SSD_HEAD_DIM = 64
D_SSM = SSD_HEADS * SSD_HEAD_DIM
SSD_GROUPS = 2
SSD_HEADS_PER_GROUP = SSD_HEADS // SSD_GROUPS
SSD_STATE = 128
CONV_WIDTH = 4
CONV_DIM = D_SSM + 2 * SSD_GROUPS * SSD_STATE
MLA_HEADS = 8
QK_NOPE_DIM = 128
QK_ROPE_DIM = 64
V_HEAD_DIM = 128
Q_LORA_RANK = 384
KV_LORA_RANK = 256
D_MLA = MLA_HEADS * V_HEAD_DIM
D_MIX = D_SSM + D_MLA
ROPE_THETA = 10000.0
IN_WIDTHS = (D_SSM, CONV_DIM, SSD_HEADS, Q_LORA_RANK, KV_LORA_RANK, QK_ROPE_DIM)
D_IN_PROJ = D_SSM + CONV_DIM + SSD_HEADS + Q_LORA_RANK + KV_LORA_RANK + QK_ROPE_DIM

kernel_name = "hybrid_ssd_mla_macaron_ple"


def rms_norm(x, w):
    xf = x.astype(jnp.float32)
    y = xf * lax.rsqrt(jnp.mean(xf * xf, axis=-1, keepdims=True) + EPS)
    return (y * w.astype(jnp.float32)).astype(x.dtype)


def swiglu_ffn(x, w_in, w_out):
    g, u = jnp.split(x @ w_in, 2, axis=-1)
    return (jax.nn.silu(g) * u) @ w_out


def split_in_proj(zx):
    pieces, start = [], 0
    for w in IN_WIDTHS:
        pieces.append(zx[..., start:start + w])
        start += w
    return pieces


def causal_depthwise_conv(x, w, b):
    y = lax.conv_general_dilated(
        x, w[:, None, :].astype(x.dtype), window_strides=(1,),
        padding=[(CONV_WIDTH - 1, 0)], dimension_numbers=('NWC', 'WIO', 'NWC'),
        feature_group_count=x.shape[-1])
    return y + b


def rope_tables(positions):
    inv = ROPE_THETA ** (-jnp.arange(0, QK_ROPE_DIM, 2, dtype=jnp.float32) / QK_ROPE_DIM)
    ang = positions.astype(jnp.float32)[..., None] * inv
    return jnp.cos(ang), jnp.sin(ang)


def apply_rope(x, cos, sin):
    xf = x.astype(jnp.float32)
    x1, x2 = jnp.split(xf, 2, axis=-1)
    return jnp.concatenate([x1 * cos - x2 * sin, x2 * cos + x1 * sin], axis=-1).astype(x.dtype)


def ssd_mixer(z, xbc_raw, dt_raw, conv_w, conv_b, dt_bias, a_log, d_skip, norm_w):
    b, s, _ = z.shape
    nc = s // CHUNK
    G, HG, P, N = SSD_GROUPS, SSD_HEADS_PER_GROUP, SSD_HEAD_DIM, SSD_STATE
    xbc = jax.nn.silu(causal_depthwise_conv(xbc_raw, conv_w, conv_b)).astype(jnp.float32)
    xs = xbc[..., :D_SSM].reshape(b, nc, CHUNK, G, HG, P)
    Bm = xbc[..., D_SSM:D_SSM + G * N].reshape(b, nc, CHUNK, G, N)
    Cm = xbc[..., D_SSM + G * N:].reshape(b, nc, CHUNK, G, N)
    dt = jax.nn.softplus(dt_raw.astype(jnp.float32) + dt_bias.astype(jnp.float32))
    dt = dt.reshape(b, nc, CHUNK, G, HG)
    A = -jnp.exp(a_log.astype(jnp.float32)).reshape(G, HG)
    a_cum = jnp.cumsum(dt * A, axis=2)
    xdt = xs * dt[..., None]
    seg = a_cum[:, :, :, None] - a_cum[:, :, None, :]
    tril = jnp.tril(jnp.ones((CHUNK, CHUNK), dtype=bool))[:, :, None, None]
    decay = jnp.exp(jnp.where(tril, seg, -jnp.inf))
    cb = jnp.einsum('bclgn,bcsgn->bclsg', Cm, Bm)
    y_diag = jnp.einsum('bclsgh,bcsghp->bclghp', cb[..., None] * decay, xdt)
    decay_to_end = jnp.exp(a_cum[:, :, -1:] - a_cum)
    states = jnp.einsum('bclgn,bclghp->bcghpn', Bm, xdt * decay_to_end[..., None])
    chunk_decay = jnp.exp(a_cum[:, :, -1])

    def step(h, inp):
        st, dec = inp
        return h * dec[..., None, None] + st, h

    h0 = jnp.zeros((b, G, HG, P, N), jnp.float32)
    _, prev = lax.scan(step, h0, (jnp.moveaxis(states, 1, 0), jnp.moveaxis(chunk_decay, 1, 0)))
    prev = jnp.moveaxis(prev, 0, 1)
    y_off = jnp.einsum('bclgn,bcghpn->bclghp', Cm, prev) * jnp.exp(a_cum)[..., None]
    y = y_diag + y_off + xs * d_skip.astype(jnp.float32).reshape(G, HG)[..., None]
    y = y.reshape(b, s, D_SSM) * jax.nn.silu(z.astype(jnp.float32))
    yg = y.reshape(b, s, G, D_SSM // G)
    yg = yg * lax.rsqrt(jnp.mean(yg * yg, axis=-1, keepdims=True) + EPS)
    return (yg.reshape(b, s, D_SSM) * norm_w.astype(jnp.float32)).astype(z.dtype)


def mla_mixer(q_a, kv_a, k_rope_raw, cos, sin, q_a_norm, w_q_b, kv_a_norm, w_kv_b):
    b, s, _ = q_a.shape
    q = (rms_norm(q_a, q_a_norm) @ w_q_b).reshape(b, s, MLA_HEADS, QK_NOPE_DIM + QK_ROPE_DIM)
    q_nope = q[..., :QK_NOPE_DIM]
    q_rope = apply_rope(q[..., QK_NOPE_DIM:], cos[:, :, None], sin[:, :, None])
    kv = (rms_norm(kv_a, kv_a_norm) @ w_kv_b).reshape(b, s, MLA_HEADS, QK_NOPE_DIM + V_HEAD_DIM)
    k_nope, v = kv[..., :QK_NOPE_DIM], kv[..., QK_NOPE_DIM:]
    k_rope = apply_rope(k_rope_raw, cos, sin)
    scale = (QK_NOPE_DIM + QK_ROPE_DIM) ** -0.5
    outs = []
    for qb in range(s // Q_BLOCK):
        q0 = qb * Q_BLOCK
        k_end = q0 + Q_BLOCK
        sc = (jnp.einsum('bqhd,bkhd->bhqk', q_nope[:, q0:k_end], k_nope[:, :k_end])
              + jnp.einsum('bqhr,bkr->bhqk', q_rope[:, q0:k_end], k_rope[:, :k_end]))
        sc = sc.astype(jnp.float32) * scale
        q_chunk = (q0 + jnp.arange(Q_BLOCK)) // CHUNK
        k_chunk = jnp.arange(k_end) // CHUNK
        mask = k_chunk[None, :] <= q_chunk[:, None]
        probs = jax.nn.softmax(jnp.where(mask, sc, -jnp.inf), axis=-1).astype(v.dtype)
        outs.append(jnp.einsum('bhqk,bkhd->bqhd', probs, v[:, :k_end]))
    return jnp.concatenate(outs, axis=1).reshape(b, s, D_MLA)


def _normal(k, shape, fan_in):
    return jax.random.normal(k, shape, jnp.float32) * fan_in ** -0.5


def _gain(k, shape):
    return 1.0 + 0.02 * jax.random.normal(k, shape, jnp.float32)


def setup_inputs(seed: int = 0) -> dict:
    key = jax.random.key(seed)
    ks = jax.random.split(key, 32)
    L = DEPTH
    x = jax.random.normal(ks[0], (BATCH, SEQ, D_MODEL), jnp.float32)
    p = jax.random.normal(ks[1], (DEPTH, BATCH, SEQ, PLE_DIM), jnp.float32)
    offset = jax.random.randint(ks[2], (BATCH, 1), 0, 65536, dtype=jnp.int32)
    positions = (offset + jnp.arange(SEQ, dtype=jnp.int32)[None, :]).astype(jnp.int32)
    dt_init = jnp.exp(jax.random.uniform(ks[3], (L, SSD_HEADS), jnp.float32,
                                         minval=math.log(1e-3), maxval=math.log(1e-1)))
    dt_bias = dt_init + jnp.log(-jnp.expm1(-dt_init))
    a_log = jnp.log(jax.random.uniform(ks[4], (L, SSD_HEADS), jnp.float32, minval=1.0, maxval=16.0))
    return {
        "x": x,
        "p": p,
        "positions": positions,
        "ffn1_norm": _gain(ks[5], (L, D_MODEL)),
        "ffn1_w_in": _normal(ks[6], (L, D_MODEL, 2 * D_FF), D_MODEL),
        "ffn1_w_out": _normal(ks[7], (L, D_FF, D_MODEL), D_FF),
        "mix_norm": _gain(ks[8], (L, D_MODEL)),
        "w_in_mix": _normal(ks[9], (L, D_MODEL, D_IN_PROJ), D_MODEL),
        "conv_w": _normal(ks[10], (L, CONV_WIDTH, CONV_DIM), CONV_WIDTH),
        "conv_b": 0.02 * jax.random.normal(ks[11], (L, CONV_DIM), jnp.float32),
        "dt_bias": dt_bias,
        "a_log": a_log,
        "d_skip": 1.0 + 0.1 * jax.random.normal(ks[12], (L, SSD_HEADS), jnp.float32),
        "ssd_norm": _gain(ks[13], (L, D_SSM)),
        "q_a_norm": _gain(ks[14], (L, Q_LORA_RANK)),
        "w_q_b": _normal(ks[15], (L, Q_LORA_RANK, MLA_HEADS * (QK_NOPE_DIM + QK_ROPE_DIM)), Q_LORA_RANK),
        "kv_a_norm": _gain(ks[16], (L, KV_LORA_RANK)),
        "w_kv_b": _normal(ks[17], (L, KV_LORA_RANK, MLA_HEADS * (QK_NOPE_DIM + V_HEAD_DIM)), KV_LORA_RANK),
        "w_out_mix": _normal(ks[18], (L, D_MIX, D_MODEL), D_MIX),
        "ffn2_norm": _gain(ks[19], (L, D_MODEL)),
        "ffn2_w_in": _normal(ks[20], (L, D_MODEL, 2 * D_FF), D_MODEL),
        "ffn2_w_out": _normal(ks[21], (L, D_FF, D_MODEL), D_FF),
        "ple_norm": _gain(ks[22], (L, D_MODEL)),
        "w_ple_gate": _normal(ks[23], (L, D_MODEL, D_MODEL), D_MODEL),
        "w_ple_proj": _normal(ks[24], (L, PLE_DIM, D_MODEL), PLE_DIM),
        "final_norm": _gain(ks[25], (D_MODEL,)),
    }


def reference(x, p, positions, ffn1_norm, ffn1_w_in, ffn1_w_out, mix_norm, w_in_mix,
              conv_w, conv_b, dt_bias, a_log, d_skip, ssd_norm, q_a_norm, w_q_b,
              kv_a_norm, w_kv_b, w_out_mix, ffn2_norm, ffn2_w_in, ffn2_w_out,
              ple_norm, w_ple_gate, w_ple_proj, final_norm):
    cos, sin = rope_tables(positions)
    h = x
    for i in range(DEPTH):
        h = h + 0.5 * swiglu_ffn(rms_norm(h, ffn1_norm[i]), ffn1_w_in[i], ffn1_w_out[i])
        u = rms_norm(h, mix_norm[i])
        z, xbc, dt_raw, q_a, kv_a, k_rope_raw = split_in_proj(u @ w_in_mix[i])
        y_ssd = ssd_mixer(z, xbc, dt_raw, conv_w[i], conv_b[i], dt_bias[i], a_log[i],
                          d_skip[i], ssd_norm[i])
        y_mla = mla_mixer(q_a, kv_a, k_rope_raw, cos, sin, q_a_norm[i], w_q_b[i],
                          kv_a_norm[i], w_kv_b[i])
        h = h + jnp.concatenate([y_ssd, y_mla.astype(y_ssd.dtype)], axis=-1) @ w_out_mix[i]
        h = h + 0.5 * swiglu_ffn(rms_norm(h, ffn2_norm[i]), ffn2_w_in[i], ffn2_w_out[i])
        gate = jax.nn.sigmoid(rms_norm(h, ple_norm[i]) @ w_ple_gate[i])
        h = h + gate * (p[i] @ w_ple_proj[i])
    return rms_norm(h, final_norm)
```

```python
import numpy as np
import concourse.bass as bass
import concourse.mybir as mybir
from concourse.bass_utils import run_bass_kernel_spmd

F32 = mybir.dt.float32
BF16 = mybir.dt.bfloat16
I32 = mybir.dt.int32
AF = mybir.ActivationFunctionType
ALU = mybir.AluOpType
AX = mybir.AxisListType

ENGS = ("pe", "act", "dve", "pool", "sp")
STRICT_SAME_ENGINE = True


class Trk:
    __slots__ = ("w", "rd_e", "rd_d")

    def __init__(self):
        self.w = None
        self.rd_e = {}
        self.rd_d = []


class DmaSem:
    __slots__ = ("h", "count", "gen", "sw")

    def __init__(self, h, sw=False):
        self.h = h
        self.count = 0
        self.gen = 0
        self.sw = sw


class Sched:
    def __init__(self, nc, esems, dsems):
        self.nc = nc
        self.esem = esems
        self.base = {e: 0 for e in ENGS}
        self.dpool = dsems
        self.reset()

    def reset(self):
        self.ops = {e: [] for e in ENGS}
        self.dfree = list(self.dpool)
        self.dused = []

    def dsem(self, sw=False):
        for i in range(len(self.dfree) - 1, -1, -1):
            if self.dfree[i].sw == sw:
                d = self.dfree.pop(i)
                self.dused.append(d)
                return d
        raise RuntimeError("out of DMA semaphores")

    def op(self, eng, fn, reads=(), writes=(), dsem=None):
        deps = []
        is_dma = dsem is not None
        for t in reads:
            if t.w is not None:
                w = t.w
                if w[0] == "e" and w[1] == eng and eng == "pe" and not is_dma:
                    pass
                else:
                    deps.append(w)
        for t in writes:
            if t.w is not None:
                w = t.w
                if w[0] == "e" and w[1] == eng and (eng == "pe" or not STRICT_SAME_ENGINE) and not is_dma:
                    pass
                else:
                    deps.append(w)
            for e2, j in t.rd_e.items():
                if e2 == eng and (eng == "pe" or not STRICT_SAME_ENGINE) and not is_dma:
                    continue
                deps.append(("e", e2, j))
            deps.extend(t.rd_d)
        idx = len(self.ops[eng])
        clear = False
        if is_dma:
            assert dsem.sw == (eng == "pool"), "DMA semaphore used on the wrong queue kind"
            dsem.count += 16
            me = ("d", dsem, dsem.count, dsem.gen)
        else:
            me = ("e", eng, idx)
        self.ops[eng].append({"fn": fn, "deps": deps, "sig": False, "dsem": dsem, "clear": clear})
        for t in reads:
            if is_dma:
                t.rd_d.append(me)
                if len(t.rd_d) > 8:
                    t.rd_d = t.rd_d[-8:]
            else:
                t.rd_e[eng] = idx
        for t in writes:
            t.w = me
            t.rd_e = {}
            t.rd_d = []
        return me

    def emit(self, name=None):
        nc = self.nc
        for e in ENGS:
            seen_e = {}
            seen_d = {}
            for o in self.ops[e]:
                waits = []
                for d in o["deps"]:
                    if d[0] == "e":
                        if seen_e.get(d[1], -1) >= d[2]:
                            continue
                        seen_e[d[1]] = d[2]
                        waits.append(d)
                        self.ops[d[1]][d[2]]["sig"] = True
                    else:
                        key = (id(d[1]), d[3])
                        if seen_d.get(key, -1) >= d[2]:
                            continue
                        seen_d[key] = d[2]
                        waits.append(d)
                o["waits"] = waits
        for e in ENGS:
            c = self.base[e]
            for o in self.ops[e]:
                if o["sig"]:
                    c += 1
                o["cnt"] = c
            self.base[e] = c
        ops = self.ops
        esem = self.esem
        dused = self.dused

        def run(e, eng):
            for o in ops[e]:
                for d in o["waits"]:
                    if d[0] == "e":
                        eng.wait_ge(esem[d[1]], ops[d[1]][d[2]]["cnt"])
                    else:
                        eng.wait_ge(d[1].h, d[2])
                if o["clear"]:
                    eng.sem_clear(o["dsem"].h)
                ins = o["fn"](eng)
                if o["dsem"] is not None:
                    ins.then_inc(o["dsem"].h, 16)
                elif o["sig"]:
                    ins.then_inc(esem[e], 1)
            if e == "sp":
                for d in dused:
                    if d.count > 0:
                        eng.wait_ge(d.h, d.count)

        with nc.Block() as block:
            @block.tensor
            def _(eng):
                run("pe", eng)

            @block.scalar
            def _(eng):
                run("act", eng)

            @block.vector
            def _(eng):
                run("dve", eng)

            @block.gpsimd
            def _(eng):
                run("pool", eng)

            @block.sync
            def _(eng):
                run("sp", eng)
        self.reset()


D = 1024
DFF = 2816
NL = 2
EPS = 1e-6
C_FFN1N, C_MIXN, C_FFN2N, C_PLEN, C_FINN = 0, 8, 16, 24, 32
C_CONVW, C_CONVB, C_SSDN, C_QAN, C_KVAN = 40, 88, 100, 108, 111
C_DTB, C_ALOG, C_DSK = 113, 129, 145
NPV = 164
K_ID, K_TRI, K_U1, K_INV, K_SGN = 0, 128, 256, 384, 385
NCST = 388
TWO_PI = float(np.float32(2.0 * np.pi))
PI = float(np.pi)
SM_SCALE = float(192 ** -0.5)


class Builder:
    def __init__(self, nc, T):
        import contextlib
        self.contextlib = contextlib
        self.nc = nc
        self.T = T
        self.NT = T // 512
        self.NB = T // 128
        dt = nc.dram_tensor
        self.xT = dt("xT", [T // 512, 128, 8, 512], F32, kind="ExternalInput").ap()
        self.pT = dt("pT", [NL, 256, T], F32, kind="ExternalInput").ap()
        self.pos = dt("pos", [1, T], I32, kind="ExternalInput").ap()
        self.w = {}
        for nm, shp in (("ffn1_w_in", [NL, D, 2 * DFF]), ("ffn1_w_out", [NL, DFF, D]),
                        ("w_in_mix", [NL, D, 3280]), ("w_q_b", [NL, 384, 1536]),
                        ("w_kv_b", [NL, 256, 2048]), ("w_out_mix", [NL, 2048, D]),
                        ("ffn2_w_in", [NL, D, 2 * DFF]), ("ffn2_w_out", [NL, DFF, D]),
                        ("w_ple_gate", [NL, D, D]), ("w_ple_proj", [NL, 256, D])):
            self.w[nm] = dt(nm, shp, F32, kind="ExternalInput").ap()
        self.pv = dt("pv", [NL, 128, NPV], F32, kind="ExternalInput").ap()
        self.cst = dt("cst", [128, NCST], F32, kind="ExternalInput").ap()
        self.outT = dt("outT", [T // 512, 128, 8, 512], F32, kind="ExternalOutput").ap()
        self.hT = dt("hT", [T // 512, 128, 8, 512], F32).ap()
        self.aT = dt("aT", [T // 512, 128, DFF // 128, 512], BF16).ap()
        self.zxT = dt("zxT", [3328, T], F32).ap()
        self.dtr = dt("dtr", [T, 16], F32).ap()
        self.xbcT = dt("xbcT", [1536, T], BF16).ap()
        self.yT = dt("yT", [T // 512, 128, 16, 512], BF16).ap()
        self.ropeT = dt("ropeT", [2, 64, T], F32).ap()
        self.xnT = dt("xnT", [T // 512, 128, 8, 512], BF16).ap()
        self._n = 0

    def setup(self, es):
        nc = self.nc
        esems = {e: es.enter_context(nc.semaphore("es_" + e)) for e in ENGS}
        dsems = [DmaSem(es.enter_context(nc.semaphore("ds%d" % i)), sw=(i >= 36)) for i in range(36 + 30)]
        self.S = Sched(nc, esems, dsems)

    def phase(self):
        b = self

        class _P:
            def __enter__(s):
                s.es = b.contextlib.ExitStack()
                s.es.__enter__()
                b.es = s.es
                return s

            def __exit__(s, *a):
                if a[0] is None:
                    b.S.emit()
                return s.es.__exit__(*a)
        return _P()

    def sb(self, shape, dtype, name=None):
        self._n += 1
        return self.es.enter_context(self.nc.sbuf_tensor(name or ("sb%d" % self._n), list(shape), dtype))

    def ps(self, shape, dtype=F32, name=None):
        self._n += 1
        return self.es.enter_context(self.nc.psum_tensor(name or ("ps%d" % self._n), list(shape), dtype))

    def dma(self, q, out, in_, reads=(), writes=(), dsem=None):
        return self.S.op(q, lambda e: e.dma_start(out=out, in_=in_), reads=reads, writes=writes, dsem=dsem)

    def load_consts(self):
        pass

    def norm_phase(self, src, l, gcol, xn, xtrk, pvt, pvtrk, ones_b, otrk, epsb, final_out=None):
        S = self.S
        NT = self.NT
        hb = [(self.sb([128, 8, 512], F32), Trk(), S.dsem()) for _ in range(2)]
        sq = [(self.sb([128, 8, 512], BF16), Trk()) for _ in range(2)]
        rs = [(self.sb([128, 512], F32), Trk()) for _ in range(2)]
        pss = [(self.ps([128, 512]), Trk()) for _ in range(2)]
        ob = None
        if final_out is not None:
            ob = [(self.sb([128, 8, 512], F32), Trk(), S.dsem()) for _ in range(2)]
        for t in range(NT):
            h, ht, hd = hb[t % 2]
            q, qt = sq[t % 2]
            r, rt = rs[t % 2]
            p, pt = pss[t % 2]
            ts_ = slice(t * 512, (t + 1) * 512)
            self.dma("sp", h[:], src[t], writes=[ht], dsem=hd)
            S.op("act", lambda e, q=q, h=h: e.activation(out=q[:], in_=h[:], func=AF.Square), reads=[ht], writes=[qt])
            for k in range(8):
                S.op("pe", lambda e, p=p, q=q, k=k: e.matmul(p[:], lhsT=ones_b[:], rhs=q[:, k, :], start=(k == 0), stop=(k == 7)),
                     reads=[qt, otrk], writes=[pt])
            S.op("act", lambda e, r=r, p=p: e.activation(out=r[:], in_=p[:], func=AF.Ln, bias=epsb[:], scale=1.0 / D), reads=[pt, otrk], writes=[rt])
            S.op("act", lambda e, r=r: e.activation(out=r[:], in_=r[:], func=AF.Exp, scale=-0.5), reads=[rt], writes=[rt])
            if final_out is None:
                for k in range(8):
                    S.op("dve", lambda e, h=h, r=r, k=k, ts_=ts_: e.scalar_tensor_tensor(
                        out=xn[:, k, ts_], in0=h[:, k, :], scalar=pvt[:, gcol + k:gcol + k + 1], in1=r[:], op0=ALU.mult, op1=ALU.mult),
                        reads=[ht, rt, pvtrk], writes=[xtrk[t]])
            else:
                o, ot, od = ob[t % 2]
                for k in range(8):
                    S.op("dve", lambda e, h=h, r=r, k=k, o=o: e.scalar_tensor_tensor(
                        out=o[:, k, :], in0=h[:, k, :], scalar=pvt[:, gcol + k:gcol + k + 1], in1=r[:], op0=ALU.mult, op1=ALU.mult),
                        reads=[ht, rt, pvtrk], writes=[ot])
                self.dma("sp", final_out[t], o[:], reads=[ot], dsem=od)

    def xn_load(self, xn, xtrk):
        for t in range(self.NT):
            self.dma("sp", xn[:, :, t * 512:(t + 1) * 512], self.xnT[t], writes=[xtrk[t]], dsem=self.S.dsem())

    def norm_epilogue_setup(self, c):
        S = self.S
        c["nsq"] = [(self.sb([128, 8, 512], BF16), Trk()) for _ in range(2)]
        c["nrs"] = [(self.sb([128, 512], F32), Trk()) for _ in range(2)]
        c["nxo"] = [(self.sb([128, 8, 512], BF16), Trk(), S.dsem()) for _ in range(2)]
        c["npn"] = (self.ps([128, 512]), Trk())

    def norm_epilogue(self, c, h, ht, t, gcol):
        S = self.S
        pvt, pvtrk, ones_b, otrk, epsb = c["pvt"], c["pvtrk"], c["ones_b"], c["otrk"], c["epsb"]
        q, qt = c["nsq"][t % 2]
        r, rt = c["nrs"][t % 2]
        o, ot, od = c["nxo"][t % 2]
        p, pt = c["npn"]
        S.op("act", lambda e: e.activation(out=q[:], in_=h[:], func=AF.Square), reads=[ht], writes=[qt])
        for k in range(8):
            S.op("pe", lambda e, k=k: e.matmul(p[:], lhsT=ones_b[:], rhs=q[:, k, :], start=(k == 0), stop=(k == 7)), reads=[qt, otrk], writes=[pt])
        S.op("act", lambda e: e.activation(out=r[:], in_=p[:], func=AF.Ln, bias=epsb[:], scale=1.0 / D), reads=[pt, otrk], writes=[rt])
        S.op("act", lambda e: e.activation(out=r[:], in_=r[:], func=AF.Exp, scale=-0.5), reads=[rt], writes=[rt])
        for k in range(8):
            S.op("dve", lambda e, k=k: e.scalar_tensor_tensor(
                out=o[:, k, :], in0=h[:, k, :], scalar=pvt[:, gcol + k:gcol + k + 1], in1=r[:], op0=ALU.mult, op1=ALU.mult),
                reads=[ht, rt, pvtrk], writes=[ot])
        self.dma("sp", self.xnT[t], o[:], reads=[ot], dsem=od)

    def common(self, l, need_cst=False):
        S = self.S
        c = {}
        c["pvt"] = self.sb([128, NPV], F32)
        c["pvtrk"] = Trk()
        self.dma("sp", c["pvt"][:], self.pv[l], writes=[c["pvtrk"]], dsem=S.dsem())
        c["ones_b"] = self.sb([128, 128], BF16)
        c["epsb"] = self.sb([128, 1], F32)
        c["otrk"] = Trk()
        S.op("pool", lambda e: e.memset(c["ones_b"][:], 1.0), writes=[c["otrk"]])
        S.op("pool", lambda e: e.memset(c["epsb"][:], EPS), writes=[c["otrk"]])
        if need_cst:
            c["cst"] = self.sb([128, NCST], F32)
            c["ctrk"] = Trk()
            self.dma("sp", c["cst"][:], self.cst, writes=[c["ctrk"]], dsem=S.dsem())
        return c

    def ffn(self, l, which, hsrc, prenormed=False):
        S = self.S
        T, NT = self.T, self.NT
        w_in = self.w["ffn%d_w_in" % which][l]
        w_out = self.w["ffn%d_w_out" % which][l]
        gcol = C_FFN1N if which == 1 else C_FFN2N
        with self.phase():
            c = self.common(l)
            xn = self.sb([128, 8, T], BF16)
            xtrk = [Trk() for _ in range(NT)]
            if prenormed:
                self.xn_load(xn, xtrk)
            else:
                self.norm_phase(hsrc, l, gcol, xn, xtrk, c["pvt"], c["pvtrk"], c["ones_b"], c["otrk"], c["epsb"])
            NWB = 4
            wg = [(self.sb([128, 8, 256], BF16), Trk(), S.dsem(sw=True)) for _ in range(NWB)]
            wu = [(self.sb([128, 8, 256], BF16), Trk(), S.dsem(sw=True)) for _ in range(NWB)]
            pg = [(self.ps([128, 512]), Trk()) for _ in range(2)]
            pu = [(self.ps([128, 512]), Trk()) for _ in range(2)]
            sg = [(self.sb([128, 512], F32), Trk()) for _ in range(2)]
            ast = [(self.sb([128, T], BF16), Trk(), S.dsem()) for _ in range(3)]
            w_in_v = w_in.rearrange("(kc p) n -> p kc n", p=128)
            NBLK = DFF // 256

            def loadw(b):
                g, gt, gd = wg[b % NWB]
                u, ut, ud = wu[b % NWB]
                self.dma("pool", g[:], w_in_v[:, :, b * 256:(b + 1) * 256], writes=[gt], dsem=gd)
                self.dma("pool", u[:], w_in_v[:, :, DFF + b * 256:DFF + (b + 1) * 256], writes=[ut], dsem=ud)
            loadw(0)
            loadw(1)
            it = 0
            for b in range(NBLK):
                if b + 2 < NBLK:
                    loadw(b + 2)
                g, gt, gd = wg[b % NWB]
                u, ut, ud = wu[b % NWB]
                for cc in range(2):
                    j = b * 2 + cc
                    a, at, ad = ast[j % 3]
                    for t in range(NT):
                        ts_ = slice(t * 512, (t + 1) * 512)
                        p1, p1t = pg[it % 2]
                        p2, p2t = pu[it % 2]
                        s1, s1t = sg[it % 2]
                        it += 1
                        for k in range(8):
                            S.op("pe", lambda e, p1=p1, g=g, k=k, cc=cc, ts_=ts_: e.matmul(
                                p1[:], lhsT=g[:, k, cc * 128:(cc + 1) * 128], rhs=xn[:, k, ts_], start=(k == 0), stop=(k == 7)),
                                reads=[gt, xtrk[t]], writes=[p1t])
                        for k in range(8):
                            S.op("pe", lambda e, p2=p2, u=u, k=k, cc=cc, ts_=ts_: e.matmul(
                                p2[:], lhsT=u[:, k, cc * 128:(cc + 1) * 128], rhs=xn[:, k, ts_], start=(k == 0), stop=(k == 7)),
                                reads=[ut, xtrk[t]], writes=[p2t])
                        S.op("act", lambda e, s1=s1, p1=p1: e.activation(out=s1[:], in_=p1[:], func=AF.Silu), reads=[p1t], writes=[s1t])
                        S.op("dve", lambda e, a=a, s1=s1, p2=p2, ts_=ts_: e.tensor_tensor(out=a[:, ts_], in0=p2[:], in1=s1[:], op=ALU.mult),
                             reads=[p2t, s1t], writes=[at])
                    self.dma("sp", self.aT[:, :, j, :].rearrange("t p c -> p t c"), a[:, :].rearrange("p (t c) -> p t c", c=512), reads=[at], dsem=ad)
        with self.phase():
            c = self.common(l)
            self.norm_epilogue_setup(c)
            next_gcol = C_MIXN if which == 1 else C_PLEN
            wo = self.sb([128, 22, D], BF16)
            wot = [Trk() for _ in range(22)]
            for j in range(22):
                self.dma("pool", wo[:, j, :], w_out[j * 128:(j + 1) * 128, :], writes=[wot[j]], dsem=S.dsem(sw=True))
            ab = [(self.sb([128, 22, 512], BF16), Trk(), S.dsem()) for _ in range(3)]
            hb = [(self.sb([128, 8, 512], F32), Trk(), S.dsem()) for _ in range(3)]
            po = [(self.ps([128, 512]), Trk()) for _ in range(4)]
            it = 0
            def load2(t):
                a, at, ad = ab[t % 3]
                h, ht, hd = hb[t % 3]
                self.dma("sp", a[:], self.aT[t], writes=[at], dsem=ad)
                self.dma("sp", h[:], hsrc[t], writes=[ht], dsem=hd)
            load2(0)
            for t in range(NT):
                ts_ = slice(t * 512, (t + 1) * 512)
                a, at, ad = ab[t % 3]
                h, ht, hd = hb[t % 3]
                if t + 1 < NT:
                    load2(t + 1)
                for m in range(8):
                    p, pt = po[it % 4]
                    it += 1
                    for j in range(22):
                        S.op("pe", lambda e, p=p, j=j, m=m, a=a: e.matmul(
                            p[:], lhsT=wo[:, j, m * 128:(m + 1) * 128], rhs=a[:, j, :], start=(j == 0), stop=(j == 21)),
                            reads=[wot[j], at], writes=[pt])
                    S.op("dve", lambda e, p=p, h=h, m=m: e.scalar_tensor_tensor(
                        out=h[:, m, :], in0=p[:], scalar=0.5, in1=h[:, m, :], op0=ALU.mult, op1=ALU.add),
                        reads=[pt, ht], writes=[ht])
                self.dma("sp", self.hT[t], h[:], reads=[ht], dsem=hd)
                if t >= 1:
                    hp_, htp_, _ = hb[(t - 1) % 3]
                    self.norm_epilogue(c, hp_, htp_, t - 1, next_gcol)
            hp_, htp_, _ = hb[(NT - 1) % 3]
            self.norm_epilogue(c, hp_, htp_, NT - 1, next_gcol)

    def rope_tables(self):
        S = self.S
        T = self.T
        with self.phase():
            c = self.common(0, need_cst=True)
            cst, ctrk = c["cst"], c["ctrk"]
            pi_ = self.sb([64, T], I32)
            ang = self.sb([64, T], F32)
            kf = self.sb([64, T], F32)
            ki = self.sb([64, T], I32)
            r = self.sb([64, T], F32)
            m = self.sb([64, T], F32)
            o = self.sb([64, T], F32)
            t = Trk()
            HI = 6.28125
            LO = 2.0 * np.pi - 6.28125
            self.dma("sp", pi_[:], self.pos.partition_broadcast(64)[:, 0, :], writes=[t], dsem=S.dsem())
            S.op("dve", lambda e: e.tensor_copy(out=ang[:], in_=pi_[:]), reads=[t], writes=[t])
            S.op("dve", lambda e: e.tensor_scalar(out=ang[:], in0=ang[:], scalar1=cst[0:64, K_INV:K_INV + 1], scalar2=None, op0=ALU.mult), reads=[t, ctrk], writes=[t])

            def wrap(buf):
                S.op("dve", lambda e: e.tensor_scalar(out=m[:], in0=buf[:], scalar1=PI, scalar2=-2.0 * PI, op0=ALU.is_gt, op1=ALU.mult), reads=[t], writes=[t])
                S.op("dve", lambda e: e.tensor_tensor(out=buf[:], in0=buf[:], in1=m[:], op=ALU.add), reads=[t], writes=[t])
                S.op("dve", lambda e: e.tensor_scalar(out=m[:], in0=buf[:], scalar1=-PI, scalar2=2.0 * PI, op0=ALU.is_lt, op1=ALU.mult), reads=[t], writes=[t])
                S.op("dve", lambda e: e.tensor_tensor(out=buf[:], in0=buf[:], in1=m[:], op=ALU.add), reads=[t], writes=[t])
                S.op("dve", lambda e: e.tensor_scalar(out=buf[:], in0=buf[:], scalar1=PI, scalar2=-PI, op0=ALU.min, op1=ALU.max), reads=[t], writes=[t])
            S.op("dve", lambda e: e.tensor_scalar(out=kf[:], in0=ang[:], scalar1=float(1.0 / (2.0 * np.pi)), scalar2=None, op0=ALU.mult), reads=[t], writes=[t])
            S.op("dve", lambda e: e.tensor_copy(out=ki[:], in_=kf[:]), reads=[t], writes=[t])
            S.op("dve", lambda e: e.tensor_copy(out=kf[:], in_=ki[:]), reads=[t], writes=[t])
            S.op("dve", lambda e: e.scalar_tensor_tensor(out=r[:], in0=kf[:], scalar=-HI, in1=ang[:], op0=ALU.mult, op1=ALU.add), reads=[t], writes=[t])
            S.op("dve", lambda e: e.scalar_tensor_tensor(out=r[:], in0=kf[:], scalar=-LO, in1=r[:], op0=ALU.mult, op1=ALU.add), reads=[t], writes=[t])
            wrap(r)
            S.op("act", lambda e: e.activation(out=o[:], in_=r[:], func=AF.Sin), reads=[t], writes=[t])
            S.op("dve", lambda e: e.tensor_scalar(out=o[:], in0=o[:], scalar1=cst[0:64, K_SGN:K_SGN + 1], scalar2=None, op0=ALU.mult), reads=[t, ctrk], writes=[t])
            self.dma("sp", self.ropeT[1], o[:], reads=[t], dsem=S.dsem())
            S.op("dve", lambda e: e.tensor_scalar(out=r[:], in0=r[:], scalar1=PI / 2.0, scalar2=None, op0=ALU.add), reads=[t], writes=[t])
            wrap(r)
            S.op("act", lambda e: e.activation(out=kf[:], in_=r[:], func=AF.Sin), reads=[t], writes=[t])
            self.dma("sp", self.ropeT[0], kf[:], reads=[t], dsem=S.dsem())

    def inproj(self, l):
        S = self.S
        T, NT, NB = self.T, self.NT, self.NB
        w = self.w["w_in_mix"][l]
        wv = w.rearrange("(kc p) n -> p kc n", p=128)
        with self.phase():
            c = self.common(l)
            pvt, pvtrk = c["pvt"], c["pvtrk"]
            xn = self.sb([128, 8, T], BF16)
            xtrk = [Trk() for _ in range(NT)]
            self.xn_load(xn, xtrk)
            blocks = [(i * 256, 256) for i in range(10)] + [(2576, 256), (2832, 128), (2960, 256), (3216, 64)]
            NWB = 3
            wb = [(self.sb([128, 8, 256], BF16), Trk(), S.dsem(sw=True)) for _ in range(NWB)]
            pp = [(self.ps([128, 512]), Trk()) for _ in range(6)]
            st = [(self.sb([128, T], F32), Trk(), S.dsem()) for _ in range(3)]
            accs = [(self.sb([128, T], F32), Trk()) for _ in range(3)]
            cob = [(self.sb([128, T], BF16), Trk(), S.dsem()) for _ in range(2)]

            def loadw(bi):
                c0, ncol = blocks[bi]
                wt, wtt, wd = wb[bi % NWB]
                self.dma("pool", wt[:, :, 0:ncol], wv[:, :, c0:c0 + ncol], writes=[wtt], dsem=wd)
            loadw(0)
            loadw(1)
            it = 0
            so = 0
            for bi, (c0, ncol) in enumerate(blocks):
                if bi + 2 < len(blocks):
                    loadw(bi + 2)
                wt, wtt, wd = wb[bi % NWB]
                for cc in range((ncol + 127) // 128):
                    M = min(128, ncol - cc * 128)
                    r0 = c0 + cc * 128
                    is_z = r0 < 1024
                    is_xbc = 1024 <= r0 < 2560
                    sbuf, stt, sd = st[so % 3]
                    so += 1
                    if is_xbc:
                        ch = (r0 - 1024) // 128
                        acc, t_acc = accs[ch % 3]
                    for t in range(NT):
                        ts_ = slice(t * 512, (t + 1) * 512)
                        p, pt = pp[it % 6]
                        for k in range(8):
                            S.op("pe", lambda e, p=p, wt=wt, k=k, cc=cc, M=M, ts_=ts_: e.matmul(
                                p[0:M, :], lhsT=wt[:, k, cc * 128:cc * 128 + M], rhs=xn[:, k, ts_], start=(k == 0), stop=(k == 7)),
                                reads=[wtt, xtrk[t]], writes=[pt])
                        if is_z:
                            S.op("act", lambda e, p=p, sbuf=sbuf, M=M, ts_=ts_: e.activation(out=sbuf[0:M, ts_], in_=p[0:M, :], func=AF.Silu), reads=[pt], writes=[stt])
                        elif is_xbc:
                            S.op("act", lambda e, p=p, sbuf=sbuf, ts_=ts_: e.activation(out=sbuf[:, ts_], in_=p[:, :], func=AF.Copy), reads=[pt], writes=[stt])
                            S.op("act", lambda e, p=p, acc=acc, ts_=ts_, ch=ch: e.activation(
                                out=acc[:, ts_], in_=p[:, :], func=AF.Identity, scale=pvt[:, C_CONVW + 3 * 12 + ch:C_CONVW + 3 * 12 + ch + 1],
                                bias=pvt[:, C_CONVB + ch:C_CONVB + ch + 1]), reads=[pt, pvtrk], writes=[t_acc])
                        elif it % 2 == 0:
                            S.op("act", lambda e, p=p, sbuf=sbuf, M=M, ts_=ts_: e.activation(out=sbuf[0:M, ts_], in_=p[0:M, :], func=AF.Copy), reads=[pt], writes=[stt])
                        else:
                            S.op("dve", lambda e, p=p, sbuf=sbuf, M=M, ts_=ts_: e.tensor_copy(out=sbuf[0:M, ts_], in_=p[0:M, :]), reads=[pt], writes=[stt])
                        it += 1
                    if not is_xbc:
                        self.dma("sp", self.zxT[r0:r0 + M, :], sbuf[0:M, :], reads=[stt], dsem=sd)
                    else:
                        o, ot, od = cob[ch % 2]
                        x = sbuf
                        for sft in (1, 2, 3):
                            wcol = C_CONVW + (3 - sft) * 12 + ch
                            S.op("dve", lambda e, x=x, sft=sft, wcol=wcol, acc=acc: e.scalar_tensor_tensor(
                                out=acc[:, sft:T], in0=x[:, 0:T - sft], scalar=pvt[:, wcol:wcol + 1], in1=acc[:, sft:T], op0=ALU.mult, op1=ALU.add),
                                reads=[stt, t_acc, pvtrk], writes=[t_acc])
                        S.op("act", lambda e, o=o, acc=acc: e.activation(out=o[:], in_=acc[:], func=AF.Silu), reads=[t_acc], writes=[ot])
                        self.dma("sp", self.xbcT[ch * 128:(ch + 1) * 128, :], o[:], reads=[ot], dsem=od)
            wdt = self.sb([128, 8, 16], BF16)
            wdtt = Trk()
            self.dma("pool", wdt[:], wv[:, :, 2560:2576], writes=[wdtt], dsem=S.dsem(sw=True))
            pd, pdt = pp[0]
            for tb in range(NB):
                for k in range(8):
                    S.op("pe", lambda e, tb=tb, k=k: e.matmul(
                        pd[:, tb * 16:(tb + 1) * 16], lhsT=xn[:, k, tb * 128:(tb + 1) * 128], rhs=wdt[:, k, :], start=(k == 0), stop=(k == 7)),
                        reads=[wdtt, xtrk[tb // 4]], writes=[pdt])
            dts = self.sb([128, NB, 16], F32)
            dtt = Trk()
            oneb = self.sb([128, 1], F32)
            S.op("pool", lambda e: e.memset(oneb[:], 1.0), writes=[dtt])
            S.op("dve", lambda e: e.tensor_tensor(
                out=dts[:], in0=pd[:, 0:NB * 16].rearrange("p (b h) -> p b h", h=16),
                in1=pvt[:, C_DTB:C_DTB + 16].unsqueeze(1).to_broadcast([128, NB, 16]), op=ALU.add), reads=[pdt, pvtrk, dtt], writes=[dtt])
            S.op("act", lambda e: e.activation(out=dts[:], in_=dts[:], func=AF.Exp), reads=[dtt], writes=[dtt])
            S.op("act", lambda e: e.activation(out=dts[:], in_=dts[:], func=AF.Ln, bias=oneb[:], scale=1.0), reads=[dtt], writes=[dtt])
            self.dma("sp", self.dtr.rearrange("(b p) h -> p b h", p=128), dts[:], reads=[dtt], dsem=S.dsem())

    def conv(self, l):
        S = self.S
        T = self.T
        with self.phase():
            c = self.common(l)
            pvt, pvtrk = c["pvt"], c["pvtrk"]
            xb = [(self.sb([128, T], F32), Trk(), S.dsem()) for _ in range(2)]
            ac = [(self.sb([128, T], F32), Trk()) for _ in range(2)]
            ob = [(self.sb([128, T], BF16), Trk(), S.dsem()) for _ in range(2)]
            for ch in range(12):
                x, xt, xd = xb[ch % 2]
                a, at = ac[ch % 2]
                o, ot, od = ob[ch % 2]
                self.dma("sp", x[:], self.zxT[1024 + ch * 128:1024 + (ch + 1) * 128, :], writes=[xt], dsem=xd)
                S.op("dve", lambda e, a=a, x=x, ch=ch: e.tensor_scalar(
                    out=a[:], in0=x[:], scalar1=pvt[:, C_CONVW + 3 * 12 + ch:C_CONVW + 3 * 12 + ch + 1],
                    scalar2=pvt[:, C_CONVB + ch:C_CONVB + ch + 1], op0=ALU.mult, op1=ALU.add), reads=[xt, pvtrk], writes=[at])
                for sft in (1, 2, 3):
                    wcol = C_CONVW + (3 - sft) * 12 + ch
                    S.op("dve", lambda e, a=a, x=x, sft=sft, wcol=wcol: e.scalar_tensor_tensor(
                        out=a[:, sft:T], in0=x[:, 0:T - sft], scalar=pvt[:, wcol:wcol + 1], in1=a[:, sft:T], op0=ALU.mult, op1=ALU.add),
                        reads=[xt, at, pvtrk], writes=[at])
                S.op("act", lambda e, a=a, o=o: e.activation(out=o[:], in_=a[:], func=AF.Silu), reads=[at], writes=[ot])
                self.dma("sp", self.xbcT[ch * 128:(ch + 1) * 128, :], o[:], reads=[ot], dsem=od)
            for ch in range(8):
                x, xt, xd = xb[ch % 2]
                self.dma("sp", x[:], self.zxT[ch * 128:(ch + 1) * 128, :], writes=[xt], dsem=xd)
                S.op("act", lambda e, x=x: e.activation(out=x[:], in_=x[:], func=AF.Silu), reads=[xt], writes=[xt])
                self.dma("sp", self.zxT[ch * 128:(ch + 1) * 128, :], x[:], reads=[xt], dsem=xd)

    def ssd(self, l):
        S = self.S
        T, NB = self.T, self.NB
        op = S.op
        with self.phase():
            c = self.common(l, need_cst=True)
            pvt, pvtrk, cst, ctrk = c["pvt"], c["pvtrk"], c["cst"], c["ctrk"]
            ones_b, otrk, epsb = c["ones_b"], c["otrk"], c["epsb"]
            K = Trk()
            identb = self.sb([128, 128], BF16)
            ones_f = self.sb([128, 128], F32)
            A_bc = self.sb([128, 16], F32)
            Dmat = self.sb([128, 16, 128], BF16)
            tri = cst[:, K_TRI:K_TRI + 128]
            u1 = cst[:, K_U1:K_U1 + 128]
            op("dve", lambda e: e.tensor_copy(out=identb[:], in_=cst[:, K_ID:K_ID + 128]), reads=[ctrk], writes=[K])
            op("pool", lambda e: e.memset(ones_f[:], 1.0), writes=[K])
            op("act", lambda e: e.activation(out=A_bc[:], in_=pvt[:, C_ALOG:C_ALOG + 16], func=AF.Exp), reads=[pvtrk], writes=[K])
            op("dve", lambda e: e.tensor_scalar(out=A_bc[:], in0=A_bc[:], scalar1=-1.0, scalar2=None, op0=ALU.mult), reads=[K], writes=[K])
            op("dve", lambda e: e.tensor_tensor(
                out=Dmat[:], in0=cst[:, K_ID:K_ID + 128].unsqueeze(1).to_broadcast([128, 16, 128]),
                in1=pvt[:, C_DSK:C_DSK + 16].unsqueeze(2).to_broadcast([128, 16, 128]), op=ALU.mult), reads=[ctrk, pvtrk], writes=[K])
            bank = [self.ps([128, 512]) for _ in range(8)]
            bt = [Trk() for _ in range(8)]
            b0v = bank[0][:, :].bitcast(BF16)
            b1v = bank[1][:, 0:128].bitcast(BF16)
            xsT = [(self.sb([128, 8, 128], BF16), Trk(), S.dsem()) for _ in range(2)]
            bcT = [(self.sb([128, 4, 128], BF16), Trk(), S.dsem()) for _ in range(3)]
            gT = [(self.sb([128, 8, 128], F32), Trk(), S.dsem()) for _ in range(2)]
            dtl = [(self.sb([128, 16], F32), Trk(), S.dsem()) for _ in range(2)]
            yo = [(self.sb([128, 8, 128], BF16), Trk(), S.dsem()) for _ in range(2)]
            xs_tok = [(self.sb([128, 1024], BF16), Trk()) for _ in range(2)]
            B_tok = [(self.sb([128, 256], BF16), Trk()) for _ in range(2)]
            eacd = [(self.sb([128, 32], F32), Trk()) for _ in range(2)]
            xdt = [(self.sb([128, 1024], BF16), Trk()) for _ in range(2)]
            xdtw = [(self.sb([128, 1024], BF16), Trk()) for _ in range(2)]
            MT = [(self.sb([128, 2048], BF16), Trk()) for _ in range(2)]
            dtA = self.sb([128, 16], F32); acs = self.sb([128, 16], F32); dif = self.sb([128, 16], F32)
            dte = self.sb([128, 16], F32); dd = self.sb([128, 16], F32)
            t_sm = Trk()
            R = self.sb([128, 2048], F32); t_R = Trk()
            E = self.sb([128, 2048], F32); t_E = Trk()
            cbm = self.sb([128, 256], F32); t_cbm = Trk()
            tmpy = self.sb([128, 1024], F32); t_tmpy = Trk()
            ytok = self.sb([128, 1024], BF16); t_ytok = Trk()
            H = self.sb([128, 1024], F32); t_H = Trk()
            prevb = self.sb([128, 1024], BF16); t_prev = Trk()
            yg = self.sb([128, 1024], F32); t_yg = Trk()
            sqy = self.sb([128, 1024], BF16); t_sq = Trk()
            rstd = self.sb([128, 256], F32); t_rs = Trk()
            op("pool", lambda e: e.memset(H[:], 0.0), writes=[t_H])
            op("pool", lambda e: e.memset(prevb[:], 0.0), writes=[t_prev])
            xbv = self.xbcT.rearrange("(k p) t -> p k t", p=128)
            zv = self.zxT[0:1024, :].rearrange("(k p) t -> p k t", p=128)
            dtv_ = self.dtr.rearrange("(b p) h -> p b h", p=128)

            def bc_h64(ap16, h0, nh):
                return ap16[:, h0:h0 + nh].unsqueeze(2).to_broadcast([128, nh, 64])

            def stageA(ci):
                G = [[]]
                def grp():
                    G.append([])
                def op(*a_, **k_):
                    G[-1].append(lambda: S.op(*a_, **k_))
                def dma(*a_, **k_):
                    G[-1].append(lambda: self.dma(*a_, **k_))
                cs = slice(ci * 128, (ci + 1) * 128)
                xs_, t_xsT, d1 = xsT[ci % 2]
                bc_, t_bc, d2 = bcT[ci % 3]
                dt_, t_dt, d4 = dtl[ci % 2]
                xst, t_xs = xs_tok[ci % 2]
                Bt, t_B = B_tok[ci % 2]
                ec, t_ec = eacd[ci % 2]
                xd, t_xdt = xdt[ci % 2]
                xw, t_xdtw = xdtw[ci % 2]
                mt, t_MT = MT[ci % 2]
                dma("sp", xs_[:], xbv[:, 0:8, cs], writes=[t_xsT], dsem=d1)
                dma("sp", bc_[:], xbv[:, 8:12, cs], writes=[t_bc], dsem=d2)
                dma("sp", dt_[:], dtv_[:, ci, :], writes=[t_dt], dsem=d4)
                grp()
                for k in range(8):
                    op("pe", lambda e, k=k: e.transpose(out=b0v[:, k * 128:(k + 1) * 128], in_=xs_[:, k, :], identity=identb[:]),
                       reads=[t_xsT, K], writes=[bt[0]])
                op("act", lambda e: e.activation(out=xst[:], in_=b0v, func=AF.Copy), reads=[bt[0]], writes=[t_xs])
                grp()
                for g in range(2):
                    op("pe", lambda e, g=g: e.transpose(out=b1v[:, g * 128:(g + 1) * 128], in_=bc_[:, g, :], identity=identb[:]),
                       reads=[t_bc, K], writes=[bt[1]])
                op("act", lambda e: e.activation(out=Bt[:], in_=b1v, func=AF.Copy), reads=[bt[1]], writes=[t_B])
                grp()
                op("dve", lambda e: e.tensor_tensor(out=dtA[:], in0=dt_[:], in1=A_bc[:], op=ALU.mult), reads=[t_dt, K], writes=[t_sm])
                op("pe", lambda e: e.matmul(bank[1][:, 128:144], lhsT=tri, rhs=dtA[:], start=True, stop=True), reads=[t_sm, ctrk], writes=[bt[1]])
                op("pe", lambda e: e.matmul(bank[1][:, 144:160], lhsT=ones_f[:], rhs=dtA[:], start=True, stop=True), reads=[t_sm, K], writes=[bt[1]])
                op("act", lambda e: e.activation(out=ec[:, 0:16], in_=bank[1][:, 128:144], func=AF.Exp), reads=[bt[1]], writes=[t_ec])
                op("act", lambda e: e.activation(out=acs[:], in_=bank[1][:, 128:144], func=AF.Copy), reads=[bt[1]], writes=[t_sm])
                op("act", lambda e: e.activation(out=ec[:, 16:32], in_=bank[1][:, 144:160], func=AF.Exp), reads=[bt[1]], writes=[t_ec])
                op("dve", lambda e: e.tensor_tensor(out=dif[:], in0=bank[1][:, 144:160], in1=acs[:], op=ALU.subtract), reads=[bt[1], t_sm], writes=[t_sm])
                grp()
                op("act", lambda e: e.activation(out=dte[:], in_=dif[:], func=AF.Exp), reads=[t_sm], writes=[t_sm])
                op("dve", lambda e: e.tensor_tensor(out=dd[:], in0=dt_[:], in1=dte[:], op=ALU.mult), reads=[t_dt, t_sm], writes=[t_sm])
                grp()
                op("pool", lambda e: e.tensor_tensor(
                    out=xd[:, :].rearrange("p (h d) -> p h d", h=16), in0=xst[:, :].rearrange("p (h d) -> p h d", h=16),
                    in1=bc_h64(dt_, 0, 16), op=ALU.mult), reads=[t_xs, t_dt], writes=[t_xdt])
                grp()
                op("pool", lambda e: e.tensor_tensor(
                    out=xw[:, :].rearrange("p (h d) -> p h d", h=16), in0=xst[:, :].rearrange("p (h d) -> p h d", h=16),
                    in1=bc_h64(dd, 0, 16), op=ALU.mult), reads=[t_xs, t_sm], writes=[t_xdtw])
                grp()
                op("dve", lambda e: e.tensor_tensor(
                    out=R[:, :].rearrange("p (h l) -> p h l", h=16), in0=tri.unsqueeze(1).to_broadcast([128, 16, 128]),
                    in1=dtA[:, :].unsqueeze(2).to_broadcast([128, 16, 128]), op=ALU.mult), reads=[ctrk, t_sm], writes=[t_R])
                for q in range(4):
                    bq = 2 + (q % 2)
                    grp()
                    op("pe", lambda e, q=q, bq=bq: e.matmul(bank[bq][:, :], lhsT=u1, rhs=R[:, q * 512:(q + 1) * 512], start=True, stop=True),
                       reads=[t_R, ctrk], writes=[bt[bq]])
                    op("act", lambda e, q=q, bq=bq: e.activation(out=E[:, q * 512:(q + 1) * 512], in_=bank[bq][:, :], func=AF.Exp),
                       reads=[bt[bq]], writes=[t_E])
                grp()
                for g in range(2):
                    op("pe", lambda e, g=g: e.matmul(bank[1][:, 256 + g * 128:256 + (g + 1) * 128], lhsT=bc_[:, g, :], rhs=bc_[:, 2 + g, :], start=True, stop=True),
                       reads=[t_bc], writes=[bt[1]])
                op("dve", lambda e: e.tensor_tensor(
                    out=cbm[:, :].rearrange("p (g l) -> p g l", g=2), in0=bank[1][:, 256:512].rearrange("p (g l) -> p g l", g=2),
                    in1=tri.unsqueeze(1).to_broadcast([128, 2, 128]), op=ALU.mult), reads=[bt[1], ctrk], writes=[t_cbm])
                grp()
                op("dve", lambda e: e.tensor_tensor(
                    out=mt[:, :].rearrange("p (g h l) -> p g h l", g=2, h=8), in0=E[:, :].rearrange("p (g h l) -> p g h l", g=2, h=8),
                    in1=cbm[:, :].rearrange("p (g l) -> p g l", g=2).unsqueeze(2).to_broadcast([128, 2, 8, 128]), op=ALU.mult),
                    reads=[t_E, t_cbm], writes=[t_MT])
                return [g_ for g_ in G if g_]

            def stageB(ci):
                G = [[]]
                def grp():
                    G.append([])
                def op(*a_, **k_):
                    G[-1].append(lambda: S.op(*a_, **k_))
                def dma(*a_, **k_):
                    G[-1].append(lambda: self.dma(*a_, **k_))
                cs = slice(ci * 128, (ci + 1) * 128)
                bc_, t_bc, d2 = bcT[ci % 3]
                g_, t_g, d3 = gT[ci % 2]
                yo_, t_yo, d5 = yo[ci % 2]
                xst, t_xs = xs_tok[ci % 2]
                Bt, t_B = B_tok[ci % 2]
                ec, t_ec = eacd[ci % 2]
                xd, t_xdt = xdt[ci % 2]
                xw, t_xdtw = xdtw[ci % 2]
                mt, t_MT = MT[ci % 2]
                dma("sp", g_[:], zv[:, :, cs], writes=[t_g], dsem=d3)
                grp()
                for h in range(16):
                    if h % 4 == 0:
                        grp()
                    ob_ = bank[4 + h // 8][:, (h % 8) * 64:(h % 8 + 1) * 64]
                    op("pe", lambda e, h=h, ob_=ob_: e.matmul(ob_, lhsT=mt[:, h * 128:(h + 1) * 128], rhs=xd[:, h * 64:(h + 1) * 64], start=True, stop=False),
                       reads=[t_MT, t_xdt], writes=[bt[4 + h // 8]])
                    op("pe", lambda e, h=h, ob_=ob_: e.matmul(ob_, lhsT=Dmat[:, h, :], rhs=xst[:, h * 64:(h + 1) * 64], start=False, stop=True),
                       reads=[K, t_xs], writes=[bt[4 + h // 8]])
                grp()
                for g in range(2):
                    op("pe", lambda e, g=g: e.matmul(bank[6 + g][:, :], lhsT=bc_[:, 2 + g, :], rhs=prevb[:, g * 512:(g + 1) * 512], start=True, stop=True),
                       reads=[t_bc, t_prev], writes=[bt[6 + g]])
                for g in range(2):
                    hs = slice(g * 512, (g + 1) * 512)
                    grp()
                    op("dve", lambda e, g=g, hs=hs: e.tensor_tensor(
                        out=tmpy[:, hs].rearrange("p (h d) -> p h d", h=8), in0=bank[6 + g][:, :].rearrange("p (h d) -> p h d", h=8),
                        in1=bc_h64(ec, g * 8, 8), op=ALU.mult), reads=[bt[6 + g], t_ec], writes=[t_tmpy])
                    op("dve", lambda e, g=g, hs=hs: e.tensor_tensor(out=ytok[:, hs], in0=bank[4 + g][:, :], in1=tmpy[:, hs], op=ALU.add),
                       reads=[bt[4 + g], t_tmpy], writes=[t_ytok])
                for g in range(2):
                    hs = slice(g * 512, (g + 1) * 512)
                    grp()
                    op("pe", lambda e, g=g, hs=hs: e.matmul(bank[6 + g][:, :], lhsT=Bt[:, g * 128:(g + 1) * 128], rhs=xw[:, hs], start=True, stop=True),
                       reads=[t_B, t_xdtw], writes=[bt[6 + g]])
                    op("dve", lambda e, g=g, hs=hs: e.tensor_tensor(
                        out=H[:, hs].rearrange("p (h d) -> p h d", h=8), in0=H[:, hs].rearrange("p (h d) -> p h d", h=8),
                        in1=bc_h64(ec, 16 + g * 8, 8), op=ALU.mult), reads=[t_H, t_ec], writes=[t_H])
                    op("dve", lambda e, g=g, hs=hs: e.tensor_tensor(out=H[:, hs], in0=bank[6 + g][:, :], in1=H[:, hs], op=ALU.add),
                       reads=[bt[6 + g], t_H], writes=[t_H])
                grp()
                op("act", lambda e: e.activation(out=prevb[:], in_=H[:], func=AF.Copy), reads=[t_H], writes=[t_prev])
                grp()
                for k in range(8):
                    op("pe", lambda e, k=k: e.transpose(out=b0v[:, k * 128:(k + 1) * 128], in_=ytok[:, k * 128:(k + 1) * 128], identity=identb[:]),
                       reads=[t_ytok, K], writes=[bt[0]])
                op("dve", lambda e: e.tensor_tensor(out=yg[:], in0=b0v, in1=g_[:, :, :].rearrange("p k t -> p (k t)"), op=ALU.mult),
                   reads=[bt[0], t_g], writes=[t_yg])
                grp()
                op("act", lambda e: e.activation(out=sqy[:], in_=yg[:], func=AF.Square), reads=[t_yg], writes=[t_sq])
                grp()
                for g in range(2):
                    for k4 in range(4):
                        op("pe", lambda e, g=g, k4=k4: e.matmul(bank[1][:, g * 128:(g + 1) * 128], lhsT=ones_b[:], rhs=sqy[:, (4 * g + k4) * 128:(4 * g + k4 + 1) * 128],
                                                               start=(k4 == 0), stop=(k4 == 3)), reads=[t_sq, otrk], writes=[bt[1]])
                op("act", lambda e: e.activation(out=rstd[:], in_=bank[1][:, 0:256], func=AF.Ln, bias=epsb[:], scale=1.0 / 512), reads=[bt[1], otrk], writes=[t_rs])
                grp()
                op("act", lambda e: e.activation(out=rstd[:], in_=rstd[:], func=AF.Exp, scale=-0.5), reads=[t_rs], writes=[t_rs])
                grp()
                op("pool", lambda e: e.tensor_tensor(
                    out=yg[:, :].rearrange("p (k t) -> p k t", k=8), in0=yg[:, :].rearrange("p (k t) -> p k t", k=8),
                    in1=pvt[:, C_SSDN:C_SSDN + 8].unsqueeze(2).to_broadcast([128, 8, 128]), op=ALU.mult), reads=[t_yg, pvtrk, t_sq], writes=[t_yg])
                grp()
                op("dve", lambda e: e.tensor_tensor(
                    out=yo_[:, :, :].rearrange("p (g k) t -> p g k t", g=2), in0=yg[:, :].rearrange("p (g k t) -> p g k t", g=2, k=4),
                    in1=rstd[:, :].rearrange("p (g t) -> p g t", g=2).unsqueeze(2).to_broadcast([128, 2, 4, 128]), op=ALU.mult),
                    reads=[t_yg, t_rs], writes=[t_yo])
                dma("sp", self.yT[ci // 4, :, 0:8, (ci % 4) * 128:(ci % 4 + 1) * 128], yo_[:], reads=[t_yo], dsem=d5)
                return [g_ for g_ in G if g_]

            def run(groups):
                for g_ in groups:
                    for th in g_:
                        th()

            def interleave(ga, gb):
                na, nb = len(ga), len(gb)
                i = j = 0
                while i < na or j < nb:
                    if j >= nb or (i < na and i * nb <= j * na):
                        run([ga[i]])
                        i += 1
                    else:
                        run([gb[j]])
                        j += 1
            run(stageA(0))
            for ci in range(NB):
                gb = stageB(ci)
                if ci + 1 < NB:
                    interleave(stageA(ci + 1), gb)
                else:
                    run(gb)

    def mla(self, l):
        S = self.S
        T, NT, NB = self.T, self.NT, self.NB
        op = S.op
        wq = self.w["w_q_b"][l].rearrange("(kc p) n -> p kc n", p=128)
        wkv = self.w["w_kv_b"][l].rearrange("(kc p) n -> p kc n", p=128)
        with self.phase():
            c = self.common(l)
            pvt, pvtrk, ones_b, otrk, epsb = c["pvt"], c["pvtrk"], c["ones_b"], c["otrk"], c["epsb"]
            bank = [self.ps([128, 512]) for _ in range(8)]
            bt = [Trk() for _ in range(8)]
            cos2 = self.sb([64, T], F32); sin2 = self.sb([64, T], F32); t_tab = Trk()
            self.dma("sp", cos2[:], self.ropeT[0], writes=[t_tab], dsem=S.dsem())
            self.dma("sp", sin2[:], self.ropeT[1], writes=[t_tab], dsem=S.dsem())
            qn = self.sb([128, 3, T], BF16); kvn = self.sb([128, 2, T], BF16); kr = self.sb([128, T], BF16)
            t_qn = [Trk() for _ in range(NT)]; t_kvn = [Trk() for _ in range(NT)]; t_kr = [Trk() for _ in range(NT)]
            t_pad = Trk()
            ld = [(self.sb([128, 5, 512], F32), Trk(), S.dsem()) for _ in range(2)]
            kl = [(self.sb([64, 2, 512], F32), Trk(), S.dsem()) for _ in range(2)]
            sq = self.sb([128, 5, 512], BF16); t_sq = Trk()
            rs = self.sb([128, 2, 512], F32); t_rs = Trk()
            tmp1 = self.sb([64, 512], F32); tmp2 = self.sb([64, 512], F32); t_tmp = Trk()
            for t in range(NT):
                ts_ = slice(t * 512, (t + 1) * 512)
                x_, t_x, d_x = ld[t % 2]
                k_, t_k, d_k = kl[t % 2]
                self.dma("sp", x_[:, 0:3, :], self.zxT[2576:2960, ts_].rearrange("(k p) t -> p k t", p=128), writes=[t_x], dsem=d_x)
                self.dma("sp", x_[:, 3:5, :], self.zxT[2960:3216, ts_].rearrange("(k p) t -> p k t", p=128), writes=[t_x], dsem=d_x)
                self.dma("sp", k_[:, 0, :], self.zxT[3216:3280, ts_], writes=[t_k], dsem=d_k)
                self.dma("sp", k_[0:32, 1, :], self.zxT[3248:3280, ts_], writes=[t_k], dsem=d_k)
                self.dma("sp", k_[32:64, 1, :], self.zxT[3216:3248, ts_], writes=[t_k], dsem=d_k)
                op("act", lambda e, x_=x_: e.activation(out=sq[:], in_=x_[:], func=AF.Square), reads=[t_x], writes=[t_sq])
                for k in range(3):
                    op("pe", lambda e, k=k: e.matmul(bank[7][:, :], lhsT=ones_b[:], rhs=sq[:, k, :], start=(k == 0), stop=(k == 2)), reads=[t_sq, otrk], writes=[bt[7]])
                for k in range(2):
                    op("pe", lambda e, k=k: e.matmul(bank[6][:, :], lhsT=ones_b[:], rhs=sq[:, 3 + k, :], start=(k == 0), stop=(k == 1)), reads=[t_sq, otrk], writes=[bt[6]])
                op("act", lambda e: e.activation(out=rs[:, 0, :], in_=bank[7][:, :], func=AF.Ln, bias=epsb[:], scale=1.0 / 384), reads=[bt[7], otrk], writes=[t_rs])
                op("act", lambda e: e.activation(out=rs[:, 1, :], in_=bank[6][:, :], func=AF.Ln, bias=epsb[:], scale=1.0 / 256), reads=[bt[6], otrk], writes=[t_rs])
                op("act", lambda e: e.activation(out=rs[:], in_=rs[:], func=AF.Exp, scale=-0.5), reads=[t_rs], writes=[t_rs])
                for k in range(3):
                    op("dve", lambda e, k=k, x_=x_, ts_=ts_: e.scalar_tensor_tensor(
                        out=qn[:, k, ts_], in0=x_[:, k, :], scalar=pvt[:, C_QAN + k:C_QAN + k + 1], in1=rs[:, 0, :], op0=ALU.mult, op1=ALU.mult),
                        reads=[t_x, t_rs, pvtrk], writes=[t_qn[t]])
                for k in range(2):
                    op("dve", lambda e, k=k, x_=x_, ts_=ts_: e.scalar_tensor_tensor(
                        out=kvn[:, k, ts_], in0=x_[:, 3 + k, :], scalar=pvt[:, C_KVAN + k:C_KVAN + k + 1], in1=rs[:, 1, :], op0=ALU.mult, op1=ALU.mult),
                        reads=[t_x, t_rs, pvtrk], writes=[t_kvn[t]])
                op("dve", lambda e, k_=k_, ts_=ts_: e.tensor_tensor(out=tmp1[:], in0=k_[:, 0, :], in1=cos2[:, ts_], op=ALU.mult), reads=[t_k, t_tab], writes=[t_tmp])
                op("dve", lambda e, k_=k_, ts_=ts_: e.tensor_tensor(out=tmp2[:], in0=k_[:, 1, :], in1=sin2[:, ts_], op=ALU.mult), reads=[t_k, t_tab], writes=[t_tmp])
                op("dve", lambda e, ts_=ts_: e.tensor_tensor(out=kr[0:64, ts_], in0=tmp1[:], in1=tmp2[:], op=ALU.add), reads=[t_tmp], writes=[t_kr[t]])
            wqh = [(self.sb([128, 3, 256], BF16), [Trk(), Trk(), Trk()], [S.dsem(sw=True), S.dsem(sw=True), S.dsem(sw=True)]) for _ in range(2)]
            wkh = [(self.sb([128, 2, 256], BF16), Trk(), S.dsem(sw=True)) for _ in range(2)]
            qnope = self.sb([128, T], BF16); qr = self.sb([128, T], BF16); knope = self.sb([128, T], BF16); V = self.sb([128, NB, 128], BF16)
            t_qnope = [Trk() for _ in range(NT)]; t_qr = [Trk() for _ in range(NT)]; t_kn = [Trk() for _ in range(NT)]; t_V = [Trk() for _ in range(NT)]
            PT = [(self.sb([128, 512], BF16), Trk()) for _ in range(4)]
            op("pool", lambda e: e.memset(kr[64:128, :], 0.0), writes=[t_pad])
            op("pool", lambda e: e.memset(qr[64:128, :], 0.0), writes=[t_pad])
            SB = [0, 1, 2, 7]
            rl = self.sb([128, 512], F32); t_rl = Trk()
            lacc = [self.sb([128, 512], F32) for _ in range(2)]; t_lacc = [Trk(), Trk()]
            ones_f = self.sb([128, 128], F32); t_onesf = Trk()
            op("pool", lambda e: e.memset(ones_f[:], 1.0), writes=[t_onesf])
            ost = [(self.sb([128, 512], BF16), Trk(), S.dsem()) for _ in range(2)]

            def loadw(h):
                a, ta, da = wqh[h % 2]
                b_, tb_, db = wkh[h % 2]
                q0 = h * 192
                self.dma("pool", a[:, :, 0:192], wq[:, :, q0:q0 + 192], writes=[ta[0]], dsem=da[0])
                self.dma("pool", a[:, :, 192:224], wq[:, :, q0 + 160:q0 + 192], writes=[ta[1]], dsem=da[1])
                self.dma("pool", a[:, :, 224:256], wq[:, :, q0 + 128:q0 + 160], writes=[ta[2]], dsem=da[2])
                self.dma("pool", b_[:], wkv[:, :, h * 256:(h + 1) * 256], writes=[tb_], dsem=db)
            loadw(0)
            si = 0
            oi = 0
            for h in range(8):
                if h + 1 < 8:
                    loadw(h + 1)
                a, ta, da = wqh[h % 2]
                b_, tb_, db = wkh[h % 2]
                for t in range(NT):
                    ts_ = slice(t * 512, (t + 1) * 512)
                    for k in range(3):
                        op("pe", lambda e, k=k, a=a, ts_=ts_: e.matmul(bank[0][:, :], lhsT=a[:, k, 0:128], rhs=qn[:, k, ts_], start=(k == 0), stop=(k == 2)),
                           reads=[ta[0], t_qn[t]], writes=[bt[0]])
                    op("act", lambda e, ts_=ts_: e.activation(out=qnope[:, ts_], in_=bank[0][:, :], func=AF.Copy), reads=[bt[0]], writes=[t_qnope[t]])
                    for k in range(3):
                        op("pe", lambda e, k=k, a=a, ts_=ts_: e.matmul(bank[1][0:64, :], lhsT=a[:, k, 128:192], rhs=qn[:, k, ts_], start=(k == 0), stop=(k == 2)),
                           reads=[ta[0], t_qn[t]], writes=[bt[1]])
                    for k in range(3):
                        op("pe", lambda e, k=k, a=a, ts_=ts_: e.matmul(bank[2][0:64, :], lhsT=a[:, k, 192:256], rhs=qn[:, k, ts_], start=(k == 0), stop=(k == 2)),
                           reads=[ta[1], ta[2], t_qn[t]], writes=[bt[2]])
                    op("dve", lambda e, ts_=ts_: e.tensor_tensor(out=tmp1[:], in0=bank[1][0:64, :], in1=cos2[:, ts_], op=ALU.mult), reads=[bt[1], t_tab], writes=[t_tmp])
                    op("dve", lambda e, ts_=ts_: e.tensor_tensor(out=tmp2[:], in0=bank[2][0:64, :], in1=sin2[:, ts_], op=ALU.mult), reads=[bt[2], t_tab], writes=[t_tmp])
                    op("dve", lambda e, ts_=ts_: e.tensor_tensor(out=qr[0:64, ts_], in0=tmp1[:], in1=tmp2[:], op=ALU.add), reads=[t_tmp], writes=[t_qr[t]])
                    for k in range(2):
                        op("pe", lambda e, k=k, b_=b_, ts_=ts_: e.matmul(bank[0][:, :], lhsT=b_[:, k, 0:128], rhs=kvn[:, k, ts_], start=(k == 0), stop=(k == 1)),
                           reads=[tb_, t_kvn[t]], writes=[bt[0]])
                    op("act", lambda e, ts_=ts_: e.activation(out=knope[:, ts_], in_=bank[0][:, :], func=AF.Copy), reads=[bt[0]], writes=[t_kn[t]])
                    for i in range(4):
                        blk = slice(t * 512 + i * 128, t * 512 + (i + 1) * 128)
                        for k in range(2):
                            op("pe", lambda e, k=k, b_=b_, i=i, blk=blk: e.matmul(bank[7][:, i * 128:(i + 1) * 128], lhsT=kvn[:, k, blk], rhs=b_[:, k, 128:256], start=(k == 0), stop=(k == 1)),
                               reads=[tb_, t_kvn[t]], writes=[bt[7]])
                    op("dve", lambda e, t=t: e.tensor_copy(out=V[:, t * 4:(t + 1) * 4, :].rearrange("p b d -> p (b d)"), in_=bank[7][:, :]), reads=[bt[7]], writes=[t_V[t]])
                for s_ in range(NT):
                    nkb = 4 * s_ + 4
                    ob = 3 if (oi % 2 == 0) else 5
                    slots = {}

                    def emitS(kb, s_=s_):
                        nonlocal si
                        c0 = max(0, kb - 4 * s_) * 128
                        qs = slice(s_ * 512 + c0, (s_ + 1) * 512)
                        ks = slice(kb * 128, (kb + 1) * 128)
                        sl_ = si % 4
                        sb_ = SB[sl_]
                        si += 1
                        slots[kb] = sl_
                        P_, t_P = PT[sl_]
                        op("pe", lambda e: e.matmul(bank[sb_][:, c0:512], lhsT=knope[:, ks], rhs=qnope[:, qs], start=True, stop=False),
                           reads=[t_kn[kb // 4], t_qnope[s_]], writes=[bt[sb_]])
                        op("pe", lambda e: e.matmul(bank[sb_][:, c0:512], lhsT=kr[:, ks], rhs=qr[:, qs], start=False, stop=True),
                           reads=[t_kr[kb // 4], t_qr[s_], t_pad], writes=[bt[sb_]])
                        if kb < 4 * s_:
                            op("act", lambda e: e.activation(out=P_[:], in_=bank[sb_][:, :], func=AF.Exp, scale=SM_SCALE), reads=[bt[sb_]], writes=[t_P])
                        else:
                            op("act", lambda e: e.activation(out=P_[0:64, c0:512], in_=bank[sb_][0:64, c0:512], func=AF.Exp, scale=SM_SCALE), reads=[bt[sb_]], writes=[t_P])
                            op("act", lambda e: e.activation(out=P_[64:128, c0 + 64:512], in_=bank[sb_][64:128, c0 + 64:512], func=AF.Exp, scale=SM_SCALE), reads=[bt[sb_]], writes=[t_P])
                            op("act", lambda e: e.activation(out=P_[64:128, c0:c0 + 64], in_=bank[sb_][64:128, c0:c0 + 64], func=AF.Identity, scale=0.0), reads=[bt[sb_]], writes=[t_P])

                    def emitPV(kb, s_=s_, nkb=nkb, ob=ob):
                        c0 = max(0, kb - 4 * s_) * 128
                        P_, t_P = PT[slots[kb]]
                        op("pe", lambda e: e.matmul(bank[ob][:, c0:512], lhsT=V[:, kb, :], rhs=P_[:, c0:512], start=(kb == 0), stop=(kb == nkb - 1)),
                           reads=[t_V[kb // 4], t_P], writes=[bt[ob]])
                        la_ = lacc[kb % 2]
                        tl_ = t_lacc[kb % 2]
                        if kb < 2:
                            if c0 > 0:
                                op("pool", lambda e: e.memset(la_[:, 0:c0], 0.0), writes=[tl_])
                            op("dve", lambda e: e.tensor_copy(out=la_[:, c0:512], in_=P_[:, c0:512]), reads=[t_P], writes=[tl_])
                        else:
                            op("dve", lambda e: e.tensor_tensor(out=la_[:, c0:512], in0=la_[:, c0:512], in1=P_[:, c0:512], op=ALU.add), reads=[t_P, tl_], writes=[tl_])
                    LA = 3
                    for kb in range(min(LA, nkb)):
                        emitS(kb)
                    for kb in range(nkb):
                        emitPV(kb)
                        if kb + LA < nkb:
                            emitS(kb + LA)
                    o_, t_o, d_o = ost[oi % 2]
                    oi += 1
                    op("pe", lambda e, ob=ob: e.matmul(bank[ob + 1][:, :], lhsT=ones_f[:], rhs=lacc[0][:], start=True, stop=False), reads=[t_onesf, t_lacc[0]], writes=[bt[ob + 1]])
                    op("pe", lambda e, ob=ob: e.matmul(bank[ob + 1][:, :], lhsT=ones_f[:], rhs=lacc[1][:], start=False, stop=True), reads=[t_onesf, t_lacc[1]], writes=[bt[ob + 1]])
                    op("act", lambda e, ob=ob: e.activation(out=rl[:], in_=bank[ob + 1][:, :], func=AF.Ln), reads=[bt[ob + 1]], writes=[t_rl])
                    op("act", lambda e: e.activation(out=rl[:], in_=rl[:], func=AF.Exp, scale=-1.0), reads=[t_rl], writes=[t_rl])
                    op("dve", lambda e, o_=o_, ob=ob: e.tensor_tensor(out=o_[:], in0=bank[ob][:, :], in1=rl[:], op=ALU.mult), reads=[bt[ob], t_rl], writes=[t_o])
                    self.dma("sp", self.yT[s_, :, 8 + h, :], o_[:], reads=[t_o], dsem=d_o)

    def outproj(self, l):
        S = self.S
        NT = self.NT
        w = self.w["w_out_mix"][l]
        with self.phase():
            c = self.common(l)
            self.norm_epilogue_setup(c)
            wo = self.sb([128, 16, D], BF16)
            wot = [Trk() for _ in range(16)]
            for j in range(16):
                self.dma("pool", wo[:, j, :], w[j * 128:(j + 1) * 128, :], writes=[wot[j]], dsem=S.dsem(sw=True))
            ab = [(self.sb([128, 16, 512], BF16), Trk(), S.dsem()) for _ in range(3)]
            hb = [(self.sb([128, 8, 512], F32), Trk(), S.dsem()) for _ in range(3)]
            po = [(self.ps([128, 512]), Trk()) for _ in range(4)]
            it = 0
            def load2(t):
                a, at, ad = ab[t % 3]
                h, ht, hd = hb[t % 3]
                self.dma("sp", a[:], self.yT[t], writes=[at], dsem=ad)
                self.dma("sp", h[:], self.hT[t], writes=[ht], dsem=hd)
            load2(0)
            for t in range(NT):
                ts_ = slice(t * 512, (t + 1) * 512)
                a, at, ad = ab[t % 3]
                h, ht, hd = hb[t % 3]
                if t + 1 < NT:
                    load2(t + 1)
                for m in range(8):
                    p, pt = po[it % 4]
                    it += 1
                    for j in range(16):
                        S.op("pe", lambda e, p=p, j=j, m=m, a=a: e.matmul(
                            p[:], lhsT=wo[:, j, m * 128:(m + 1) * 128], rhs=a[:, j, :], start=(j == 0), stop=(j == 15)),
                            reads=[wot[j], at], writes=[pt])
                    S.op("dve", lambda e, p=p, h=h, m=m: e.tensor_tensor(out=h[:, m, :], in0=p[:], in1=h[:, m, :], op=ALU.add),
                         reads=[pt, ht], writes=[ht])
                self.dma("sp", self.hT[t], h[:], reads=[ht], dsem=hd)
                if t >= 1:
                    hp_, htp_, _ = hb[(t - 1) % 3]
                    self.norm_epilogue(c, hp_, htp_, t - 1, C_FFN2N)
            hp_, htp_, _ = hb[(NT - 1) % 3]
            self.norm_epilogue(c, hp_, htp_, NT - 1, C_FFN2N)

    def ple(self, l):
        S = self.S
        T, NT = self.T, self.NT
        wg = self.w["w_ple_gate"][l].rearrange("(kc p) n -> p kc n", p=128)
        wp = self.w["w_ple_proj"][l].rearrange("(kc p) n -> p kc n", p=128)
        with self.phase():
            c = self.common(l)
            xn = self.sb([128, 8, T], BF16)
            xtrk = [Trk() for _ in range(NT)]
            self.xn_load(xn, xtrk)
            pb = self.sb([128, 2, T], BF16)
            t_pb = Trk()
            self.dma("pool", pb[:], self.pT[l].rearrange("(k p) t -> p k t", p=128), writes=[t_pb], dsem=S.dsem(sw=True))
            wgb = [(self.sb([128, 8, 128], BF16), Trk(), S.dsem(sw=True)) for _ in range(3)]
            wpb = [(self.sb([128, 2, 128], BF16), Trk(), S.dsem(sw=True)) for _ in range(3)]
            hb = [(self.sb([128, T], F32), Trk(), S.dsem()) for _ in range(2)]
            pg = [(self.ps([128, 512]), Trk()) for _ in range(2)]
            pp = [(self.ps([128, 512]), Trk()) for _ in range(2)]
            sg = [(self.sb([128, 512], F32), Trk()) for _ in range(2)]

            def loadw(m):
                a, ta, da = wgb[m % 3]
                b_, tb_, db = wpb[m % 3]
                self.dma("pool", a[:], wg[:, :, m * 128:(m + 1) * 128], writes=[ta], dsem=da)
                self.dma("pool", b_[:], wp[:, :, m * 128:(m + 1) * 128], writes=[tb_], dsem=db)
            def loadh(m):
                h, ht, hd = hb[m % 2]
                self.dma("sp", h[:, :].rearrange("p (t c) -> p t c", c=512), self.hT[:, :, m, :].rearrange("t p c -> p t c"), writes=[ht], dsem=hd)
            loadw(0)
            loadw(1)
            it = 0
            for m in range(8):
                if m + 2 < 8:
                    loadw(m + 2)
                a, ta, da = wgb[m % 3]
                b_, tb_, db = wpb[m % 3]
                h, ht, hd = hb[m % 2]
                if m == 0:
                    loadh(0)
                if m + 1 < 8:
                    loadh(m + 1)
                for t in range(NT):
                    ts_ = slice(t * 512, (t + 1) * 512)
                    p1, p1t = pg[it % 2]
                    p2, p2t = pp[it % 2]
                    s1, s1t = sg[it % 2]
                    it += 1
                    for k in range(8):
                        S.op("pe", lambda e, p1=p1, a=a, k=k, ts_=ts_: e.matmul(p1[:], lhsT=a[:, k, :], rhs=xn[:, k, ts_], start=(k == 0), stop=(k == 7)),
                             reads=[ta, xtrk[t]], writes=[p1t])
                    for k in range(2):
                        S.op("pe", lambda e, p2=p2, b_=b_, k=k, ts_=ts_: e.matmul(p2[:], lhsT=b_[:, k, :], rhs=pb[:, k, ts_], start=(k == 0), stop=(k == 1)),
                             reads=[tb_, t_pb], writes=[p2t])
                    S.op("act", lambda e, s1=s1, p1=p1: e.activation(out=s1[:], in_=p1[:], func=AF.Sigmoid), reads=[p1t], writes=[s1t])
                    S.op("dve", lambda e, s1=s1, p2=p2: e.tensor_tensor(out=s1[:], in0=p2[:], in1=s1[:], op=ALU.mult), reads=[p2t, s1t], writes=[s1t])
                    S.op("dve", lambda e, s1=s1, h=h, ts_=ts_: e.tensor_tensor(out=h[:, ts_], in0=h[:, ts_], in1=s1[:], op=ALU.add), reads=[s1t, ht], writes=[ht])
                self.dma("sp", self.hT[:, :, m, :].rearrange("t p c -> p t c"), h[:, :].rearrange("p (t c) -> p t c", c=512), reads=[ht], dsem=hd)

    def full(self):
        self.rope_tables()
        src = self.xT
        for l in range(NL):
            self.ffn(l, 1, src)
            src = self.hT
            self.inproj(l)
            self.ssd(l)
            self.mla(l)
            self.outproj(l)
            self.ffn(l, 2, self.hT, prenormed=True)
            self.ple(l)
        self.final(self.hT)

    def final(self, hsrc):
        with self.phase():
            c = self.common(0)
            self.norm_phase(hsrc, 0, C_FINN, None, None, c["pvt"], c["pvtrk"], c["ones_b"], c["otrk"], c["epsb"], final_out=self.outT)


def build(T=4096, stages=None):
    nc = bass.Bass("TRN2", target_bir_lowering=False)
    import contextlib
    b = Builder(nc, T)
    with contextlib.ExitStack() as es:
        b.setup(es)
        if stages == "ffn":
            b.ffn(0, 1, b.xT)
            b.final(b.hT)
        else:
            b.full()
    return nc


def pack_pv(inp):
    pv = np.zeros((NL, 128, NPV), np.float32)

    def col(v, n):
        return np.asarray(v, np.float32).reshape(n, 128).T
    for l in range(NL):
        pv[l, :, C_FFN1N:C_FFN1N + 8] = col(inp["ffn1_norm"][l], 8)
        pv[l, :, C_MIXN:C_MIXN + 8] = col(inp["mix_norm"][l], 8)
        pv[l, :, C_FFN2N:C_FFN2N + 8] = col(inp["ffn2_norm"][l], 8)
        pv[l, :, C_PLEN:C_PLEN + 8] = col(inp["ple_norm"][l], 8)
        pv[l, :, C_FINN:C_FINN + 8] = col(inp["final_norm"], 8)
        for w in range(4):
            pv[l, :, C_CONVW + w * 12:C_CONVW + (w + 1) * 12] = col(inp["conv_w"][l, w], 12)
        pv[l, :, C_CONVB:C_CONVB + 12] = col(inp["conv_b"][l], 12)
        pv[l, :, C_SSDN:C_SSDN + 8] = col(inp["ssd_norm"][l], 8)
        pv[l, :, C_QAN:C_QAN + 3] = col(inp["q_a_norm"][l], 3)
        pv[l, :, C_KVAN:C_KVAN + 2] = col(inp["kv_a_norm"][l], 2)
        pv[l, :, C_DTB:C_DTB + 16] = np.asarray(inp["dt_bias"][l], np.float32)[None, :]
        pv[l, :, C_ALOG:C_ALOG + 16] = np.asarray(inp["a_log"][l], np.float32)[None, :]
        pv[l, :, C_DSK:C_DSK + 16] = np.asarray(inp["d_skip"][l], np.float32)[None, :]
    return pv


def make_cst():
    c = np.zeros((128, NCST), np.float32)
    i = np.arange(128)
    c[:, K_ID:K_ID + 128] = np.eye(128, dtype=np.float32)
    c[:, K_TRI:K_TRI + 128] = (i[:, None] <= i[None, :]).astype(np.float32)
    c[:, K_U1:K_U1 + 128] = (i[:, None] > i[None, :]).astype(np.float32)
    inv = (np.float32(10000.0) ** (-np.arange(0, 64, 2, dtype=np.float32) / np.float32(64))).astype(np.float32)
    c[:64, K_INV] = np.concatenate([inv, inv])
    c[:32, K_SGN] = -1.0
    c[32:64, K_SGN] = 1.0
    return c


def to_stream(x):
    x = np.asarray(x, np.float32)
    T = x.shape[0]
    return np.ascontiguousarray(x.reshape(T // 512, 512, 8, 128).transpose(0, 3, 2, 1))


def from_stream(y):
    nt = y.shape[0]
    return np.ascontiguousarray(np.asarray(y).transpose(0, 3, 2, 1).reshape(nt * 512, 1024))


def core_inputs(inp, bidx):
    m = {
        "xT": to_stream(inp["x"][bidx]),
        "pT": np.ascontiguousarray(np.transpose(np.asarray(inp["p"][:, bidx], np.float32), (0, 2, 1))),
        "pos": np.ascontiguousarray(np.asarray(inp["positions"][bidx], np.int32)[None, :]),
        "pv": pack_pv(inp),
        "cst": make_cst(),
    }
    for nm in ("ffn1_w_in", "ffn1_w_out", "w_in_mix", "w_q_b", "w_kv_b", "w_out_mix",
               "ffn2_w_in", "ffn2_w_out", "w_ple_gate", "w_ple_proj"):
        m[nm] = np.ascontiguousarray(np.asarray(inp[nm], np.float32))
    return m


_NC_CACHE = {}


def kernel(**inputs):
    T = inputs["x"].shape[1]
    nb = inputs["x"].shape[0]
    if T not in _NC_CACHE:
        _NC_CACHE[T] = build(T)
    nc = _NC_CACHE[T]
    shared = core_inputs(inputs, 0)
    in_maps = []
    for b in range(nb):
        m = dict(shared)
        m["xT"] = to_stream(inputs["x"][b])
        m["pT"] = np.ascontiguousarray(np.transpose(np.asarray(inputs["p"][:, b], np.float32), (0, 2, 1)))
        m["pos"] = np.ascontiguousarray(np.asarray(inputs["positions"][b], np.int32)[None, :])
        in_maps.append(m)
    res = run_bass_kernel_spmd(nc, in_maps, core_ids=list(range(nb)))
    out = np.stack([from_stream(r["outT"]) for r in res.results], axis=0)
    return out.astype(np.float32)
```

```python
import numpy as np
import concourse.bass as bass
import concourse.mybir as mybir
from concourse.bass_utils import run_bass_kernel_spmd

F32 = mybir.dt.float32
BF16 = mybir.dt.bfloat16
I32 = mybir.dt.int32
AF = mybir.ActivationFunctionType
ALU = mybir.AluOpType
AX = mybir.AxisListType

ENGS = ("pe", "act", "dve", "pool", "sp")
STRICT_SAME_ENGINE = True


class Trk:
    __slots__ = ("w", "rd_e", "rd_d")

    def __init__(self):
        self.w = None
        self.rd_e = {}
        self.rd_d = []


class DmaSem:
    __slots__ = ("h", "count", "gen", "sw")

    def __init__(self, h, sw=False):
        self.h = h
        self.count = 0
        self.gen = 0
        self.sw = sw


class Sched:
    def __init__(self, nc, esems, dsems):
        self.nc = nc
        self.esem = esems
        self.base = {e: 0 for e in ENGS}
        self.dpool = dsems
        self.reset()

    def reset(self):
        self.ops = {e: [] for e in ENGS}
        self.dfree = list(self.dpool)
        self.dused = []

    def dsem(self, sw=False):
        for i in range(len(self.dfree) - 1, -1, -1):
            if self.dfree[i].sw == sw:
                d = self.dfree.pop(i)
                self.dused.append(d)
                return d
        raise RuntimeError("out of DMA semaphores")

    def op(self, eng, fn, reads=(), writes=(), dsem=None):
        deps = []
        is_dma = dsem is not None
        for t in reads:
            if t.w is not None:
                w = t.w
                if w[0] == "e" and w[1] == eng and eng == "pe" and not is_dma:
                    pass
                else:
                    deps.append(w)
        for t in writes:
            if t.w is not None:
                w = t.w
                if w[0] == "e" and w[1] == eng and (eng == "pe" or not STRICT_SAME_ENGINE) and not is_dma:
                    pass
                else:
                    deps.append(w)
            for e2, j in t.rd_e.items():
                if e2 == eng and (eng == "pe" or not STRICT_SAME_ENGINE) and not is_dma:
                    continue
                deps.append(("e", e2, j))
            deps.extend(t.rd_d)
        idx = len(self.ops[eng])
        clear = False
        if is_dma:
            assert dsem.sw == (eng == "pool"), "DMA semaphore used on the wrong queue kind"
            dsem.count += 16
            me = ("d", dsem, dsem.count, dsem.gen)
        else:
            me = ("e", eng, idx)
        self.ops[eng].append({"fn": fn, "deps": deps, "sig": False, "dsem": dsem, "clear": clear})
        for t in reads:
            if is_dma:
                t.rd_d.append(me)
                if len(t.rd_d) > 8:
                    t.rd_d = t.rd_d[-8:]
            else:
                t.rd_e[eng] = idx
        for t in writes:
            t.w = me
            t.rd_e = {}
            t.rd_d = []
        return me

    def emit(self, name=None):
        nc = self.nc
        for e in ENGS:
            seen_e = {}
            seen_d = {}
            for o in self.ops[e]:
                waits = []
                for d in o["deps"]:
                    if d[0] == "e":
                        if seen_e.get(d[1], -1) >= d[2]:
                            continue
                        seen_e[d[1]] = d[2]
                        waits.append(d)
                        self.ops[d[1]][d[2]]["sig"] = True
                    else:
                        key = (id(d[1]), d[3])
                        if seen_d.get(key, -1) >= d[2]:
                            continue
                        seen_d[key] = d[2]
                        waits.append(d)
                o["waits"] = waits
        for e in ENGS:
            c = self.base[e]
            for o in self.ops[e]:
                if o["sig"]:
                    c += 1
                o["cnt"] = c
            self.base[e] = c
        ops = self.ops
        esem = self.esem
        dused = self.dused

        def run(e, eng):
            for o in ops[e]:
                for d in o["waits"]:
                    if d[0] == "e":
                        eng.wait_ge(esem[d[1]], ops[d[1]][d[2]]["cnt"])
                    else:
                        eng.wait_ge(d[1].h, d[2])
                if o["clear"]:
                    eng.sem_clear(o["dsem"].h)
                ins = o["fn"](eng)
                if o["dsem"] is not None:
                    ins.then_inc(o["dsem"].h, 16)
                elif o["sig"]:
                    ins.then_inc(esem[e], 1)
            if e == "sp":
                for d in dused:
                    if d.count > 0:
                        eng.wait_ge(d.h, d.count)

        with nc.Block() as block:
            @block.tensor
            def _(eng):
                run("pe", eng)

            @block.scalar
            def _(eng):
                run("act", eng)

            @block.vector
            def _(eng):
                run("dve", eng)

            @block.gpsimd
            def _(eng):
                run("pool", eng)

            @block.sync
            def _(eng):
                run("sp", eng)
        self.reset()


D = 1024
DFF = 2816
NL = 2
EPS = 1e-6
C_FFN1N, C_MIXN, C_FFN2N, C_PLEN, C_FINN = 0, 8, 16, 24, 32
C_CONVW, C_CONVB, C_SSDN, C_QAN, C_KVAN = 40, 88, 100, 108, 111
C_DTB, C_ALOG, C_DSK = 113, 129, 145
NPV = 164
K_ID, K_TRI, K_U1, K_INV, K_SGN = 0, 128, 256, 384, 385
NCST = 388
TWO_PI = float(np.float32(2.0 * np.pi))
PI = float(np.pi)
SM_SCALE = float(192 ** -0.5)


class Builder:
    def __init__(self, nc, T):
        import contextlib
        self.contextlib = contextlib
        self.nc = nc
        self.T = T
        self.NT = T // 512
        self.NB = T // 128
        dt = nc.dram_tensor
        self.xT = dt("xT", [T // 512, 128, 8, 512], F32, kind="ExternalInput").ap()
        self.pT = dt("pT", [NL, 256, T], F32, kind="ExternalInput").ap()
        self.pos = dt("pos", [1, T], I32, kind="ExternalInput").ap()
        self.w = {}
        for nm, shp in (("ffn1_w_in", [NL, D, 2 * DFF]), ("ffn1_w_out", [NL, DFF, D]),
                        ("w_in_mix", [NL, D, 3280]), ("w_q_b", [NL, 384, 1536]),
                        ("w_kv_b", [NL, 256, 2048]), ("w_out_mix", [NL, 2048, D]),
                        ("ffn2_w_in", [NL, D, 2 * DFF]), ("ffn2_w_out", [NL, DFF, D]),
                        ("w_ple_gate", [NL, D, D]), ("w_ple_proj", [NL, 256, D])):
            self.w[nm] = dt(nm, shp, F32, kind="ExternalInput").ap()
        self.pv = dt("pv", [NL, 128, NPV], F32, kind="ExternalInput").ap()
        self.cst = dt("cst", [128, NCST], F32, kind="ExternalInput").ap()
        self.outT = dt("outT", [T // 512, 128, 8, 512], F32, kind="ExternalOutput").ap()
        self.hT = dt("hT", [T // 512, 128, 8, 512], F32).ap()
        self.aT = dt("aT", [T // 512, 128, DFF // 128, 512], BF16).ap()
        self.zxT = dt("zxT", [3328, T], F32).ap()
        self.dtr = dt("dtr", [T, 16], F32).ap()
        self.xbcT = dt("xbcT", [1536, T], BF16).ap()
        self.yT = dt("yT", [T // 512, 128, 16, 512], BF16).ap()
        self.ropeT = dt("ropeT", [2, 64, T], F32).ap()
        self.xnT = dt("xnT", [T // 512, 128, 8, 512], BF16).ap()
        self._n = 0

    def setup(self, es):
        nc = self.nc
        esems = {e: es.enter_context(nc.semaphore("es_" + e)) for e in ENGS}
        dsems = [DmaSem(es.enter_context(nc.semaphore("ds%d" % i)), sw=(i >= 36)) for i in range(36 + 30)]
        self.S = Sched(nc, esems, dsems)

    def phase(self):
        b = self

        class _P:
            def __enter__(s):
                s.es = b.contextlib.ExitStack()
                s.es.__enter__()
                b.es = s.es
                return s

            def __exit__(s, *a):
                if a[0] is None:
                    b.S.emit()
                return s.es.__exit__(*a)
        return _P()

    def sb(self, shape, dtype, name=None):
        self._n += 1
        return self.es.enter_context(self.nc.sbuf_tensor(name or ("sb%d" % self._n), list(shape), dtype))

    def ps(self, shape, dtype=F32, name=None):
        self._n += 1
        return self.es.enter_context(self.nc.psum_tensor(name or ("ps%d" % self._n), list(shape), dtype))

    def dma(self, q, out, in_, reads=(), writes=(), dsem=None):
        return self.S.op(q, lambda e: e.dma_start(out=out, in_=in_), reads=reads, writes=writes, dsem=dsem)

    def load_consts(self):
        pass

    def norm_phase(self, src, l, gcol, xn, xtrk, pvt, pvtrk, ones_b, otrk, epsb, final_out=None):
        S = self.S
        NT = self.NT
        hb = [(self.sb([128, 8, 512], F32), Trk(), S.dsem()) for _ in range(2)]
        sq = [(self.sb([128, 8, 512], BF16), Trk()) for _ in range(2)]
        rs = [(self.sb([128, 512], F32), Trk()) for _ in range(2)]
        pss = [(self.ps([128, 512]), Trk()) for _ in range(2)]
        ob = None
        if final_out is not None:
            ob = [(self.sb([128, 8, 512], F32), Trk(), S.dsem()) for _ in range(2)]
        for t in range(NT):
            h, ht, hd = hb[t % 2]
            q, qt = sq[t % 2]
            r, rt = rs[t % 2]
            p, pt = pss[t % 2]
            ts_ = slice(t * 512, (t + 1) * 512)
            self.dma("sp", h[:], src[t], writes=[ht], dsem=hd)
            S.op("act", lambda e, q=q, h=h: e.activation(out=q[:], in_=h[:], func=AF.Square), reads=[ht], writes=[qt])
            for k in range(8):
                S.op("pe", lambda e, p=p, q=q, k=k: e.matmul(p[:], lhsT=ones_b[:], rhs=q[:, k, :], start=(k == 0), stop=(k == 7)),
                     reads=[qt, otrk], writes=[pt])
            S.op("act", lambda e, r=r, p=p: e.activation(out=r[:], in_=p[:], func=AF.Ln, bias=epsb[:], scale=1.0 / D), reads=[pt, otrk], writes=[rt])
            S.op("act", lambda e, r=r: e.activation(out=r[:], in_=r[:], func=AF.Exp, scale=-0.5), reads=[rt], writes=[rt])
            if final_out is None:
                for k in range(8):
                    S.op("dve", lambda e, h=h, r=r, k=k, ts_=ts_: e.scalar_tensor_tensor(
                        out=xn[:, k, ts_], in0=h[:, k, :], scalar=pvt[:, gcol + k:gcol + k + 1], in1=r[:], op0=ALU.mult, op1=ALU.mult),
                        reads=[ht, rt, pvtrk], writes=[xtrk[t]])
            else:
                o, ot, od = ob[t % 2]
                for k in range(8):
                    S.op("dve", lambda e, h=h, r=r, k=k, o=o: e.scalar_tensor_tensor(
                        out=o[:, k, :], in0=h[:, k, :], scalar=pvt[:, gcol + k:gcol + k + 1], in1=r[:], op0=ALU.mult, op1=ALU.mult),
                        reads=[ht, rt, pvtrk], writes=[ot])
                self.dma("sp", final_out[t], o[:], reads=[ot], dsem=od)

    def xn_load(self, xn, xtrk):
        for t in range(self.NT):
            self.dma("sp", xn[:, :, t * 512:(t + 1) * 512], self.xnT[t], writes=[xtrk[t]], dsem=self.S.dsem())

    def norm_epilogue_setup(self, c):
        S = self.S
        c["nsq"] = [(self.sb([128, 8, 512], BF16), Trk()) for _ in range(2)]
        c["nrs"] = [(self.sb([128, 512], F32), Trk()) for _ in range(2)]
        c["nxo"] = [(self.sb([128, 8, 512], BF16), Trk(), S.dsem()) for _ in range(2)]
        c["npn"] = (self.ps([128, 512]), Trk())

    def norm_epilogue(self, c, h, ht, t, gcol):
        S = self.S
        pvt, pvtrk, ones_b, otrk, epsb = c["pvt"], c["pvtrk"], c["ones_b"], c["otrk"], c["epsb"]
        q, qt = c["nsq"][t % 2]
        r, rt = c["nrs"][t % 2]
        o, ot, od = c["nxo"][t % 2]
        p, pt = c["npn"]
        S.op("act", lambda e: e.activation(out=q[:], in_=h[:], func=AF.Square), reads=[ht], writes=[qt])
        for k in range(8):
            S.op("pe", lambda e, k=k: e.matmul(p[:], lhsT=ones_b[:], rhs=q[:, k, :], start=(k == 0), stop=(k == 7)), reads=[qt, otrk], writes=[pt])
        S.op("act", lambda e: e.activation(out=r[:], in_=p[:], func=AF.Ln, bias=epsb[:], scale=1.0 / D), reads=[pt, otrk], writes=[rt])
        S.op("act", lambda e: e.activation(out=r[:], in_=r[:], func=AF.Exp, scale=-0.5), reads=[rt], writes=[rt])
        for k in range(8):
            S.op("dve", lambda e, k=k: e.scalar_tensor_tensor(
                out=o[:, k, :], in0=h[:, k, :], scalar=pvt[:, gcol + k:gcol + k + 1], in1=r[:], op0=ALU.mult, op1=ALU.mult),
                reads=[ht, rt, pvtrk], writes=[ot])
        self.dma("sp", self.xnT[t], o[:], reads=[ot], dsem=od)

    def common(self, l, need_cst=False):
        S = self.S
        c = {}
        c["pvt"] = self.sb([128, NPV], F32)
        c["pvtrk"] = Trk()
        self.dma("sp", c["pvt"][:], self.pv[l], writes=[c["pvtrk"]], dsem=S.dsem())
        c["ones_b"] = self.sb([128, 128], BF16)
        c["epsb"] = self.sb([128, 1], F32)
        c["otrk"] = Trk()
        S.op("pool", lambda e: e.memset(c["ones_b"][:], 1.0), writes=[c["otrk"]])
        S.op("pool", lambda e: e.memset(c["epsb"][:], EPS), writes=[c["otrk"]])
        if need_cst:
            c["cst"] = self.sb([128, NCST], F32)
            c["ctrk"] = Trk()
            self.dma("sp", c["cst"][:], self.cst, writes=[c["ctrk"]], dsem=S.dsem())
        return c

    def ffn(self, l, which, hsrc, prenormed=False):
        S = self.S
        T, NT = self.T, self.NT
        w_in = self.w["ffn%d_w_in" % which][l]
        w_out = self.w["ffn%d_w_out" % which][l]
        gcol = C_FFN1N if which == 1 else C_FFN2N
        with self.phase():
            c = self.common(l)
            xn = self.sb([128, 8, T], BF16)
            xtrk = [Trk() for _ in range(NT)]
            if prenormed:
                self.xn_load(xn, xtrk)
            else:
                self.norm_phase(hsrc, l, gcol, xn, xtrk, c["pvt"], c["pvtrk"], c["ones_b"], c["otrk"], c["epsb"])
            NWB = 4
            wg = [(self.sb([128, 8, 256], BF16), Trk(), S.dsem(sw=True)) for _ in range(NWB)]
            wu = [(self.sb([128, 8, 256], BF16), Trk(), S.dsem(sw=True)) for _ in range(NWB)]
            pg = [(self.ps([128, 512]), Trk()) for _ in range(2)]
            pu = [(self.ps([128, 512]), Trk()) for _ in range(2)]
            sg = [(self.sb([128, 512], F32), Trk()) for _ in range(2)]
            ast = [(self.sb([128, T], BF16), Trk(), S.dsem()) for _ in range(3)]
            w_in_v = w_in.rearrange("(kc p) n -> p kc n", p=128)
            NBLK = DFF // 256

            def loadw(b):
                g, gt, gd = wg[b % NWB]
                u, ut, ud = wu[b % NWB]
                self.dma("pool", g[:], w_in_v[:, :, b * 256:(b + 1) * 256], writes=[gt], dsem=gd)
                self.dma("pool", u[:], w_in_v[:, :, DFF + b * 256:DFF + (b + 1) * 256], writes=[ut], dsem=ud)
            loadw(0)
            loadw(1)
            it = 0
            for b in range(NBLK):
                if b + 2 < NBLK:
                    loadw(b + 2)
                g, gt, gd = wg[b % NWB]
                u, ut, ud = wu[b % NWB]
                for cc in range(2):
                    j = b * 2 + cc
                    a, at, ad = ast[j % 3]
                    for t in range(NT):
                        ts_ = slice(t * 512, (t + 1) * 512)
                        p1, p1t = pg[it % 2]
                        p2, p2t = pu[it % 2]
                        s1, s1t = sg[it % 2]
                        it += 1
                        for k in range(8):
                            S.op("pe", lambda e, p1=p1, g=g, k=k, cc=cc, ts_=ts_: e.matmul(
                                p1[:], lhsT=g[:, k, cc * 128:(cc + 1) * 128], rhs=xn[:, k, ts_], start=(k == 0), stop=(k == 7)),
                                reads=[gt, xtrk[t]], writes=[p1t])
                        for k in range(8):
                            S.op("pe", lambda e, p2=p2, u=u, k=k, cc=cc, ts_=ts_: e.matmul(
                                p2[:], lhsT=u[:, k, cc * 128:(cc + 1) * 128], rhs=xn[:, k, ts_], start=(k == 0), stop=(k == 7)),
                                reads=[ut, xtrk[t]], writes=[p2t])
                        S.op("act", lambda e, s1=s1, p1=p1: e.activation(out=s1[:], in_=p1[:], func=AF.Silu), reads=[p1t], writes=[s1t])
                        S.op("dve", lambda e, a=a, s1=s1, p2=p2, ts_=ts_: e.tensor_tensor(out=a[:, ts_], in0=p2[:], in1=s1[:], op=ALU.mult),
                             reads=[p2t, s1t], writes=[at])
                    self.dma("sp", self.aT[:, :, j, :].rearrange("t p c -> p t c"), a[:, :].rearrange("p (t c) -> p t c", c=512), reads=[at], dsem=ad)
        with self.phase():
            c = self.common(l)
            self.norm_epilogue_setup(c)
            next_gcol = C_MIXN if which == 1 else C_PLEN
            wo = self.sb([128, 22, D], BF16)
            wot = [Trk() for _ in range(22)]
            for j in range(22):
                self.dma("pool", wo[:, j, :], w_out[j * 128:(j + 1) * 128, :], writes=[wot[j]], dsem=S.dsem(sw=True))
            ab = [(self.sb([128, 22, 512], BF16), Trk(), S.dsem()) for _ in range(3)]
            hb = [(self.sb([128, 8, 512], F32), Trk(), S.dsem()) for _ in range(3)]
            po = [(self.ps([128, 512]), Trk()) for _ in range(4)]
            it = 0
            def load2(t):
                a, at, ad = ab[t % 3]
                h, ht, hd = hb[t % 3]
                self.dma("sp", a[:], self.aT[t], writes=[at], dsem=ad)
                self.dma("sp", h[:], hsrc[t], writes=[ht], dsem=hd)
            load2(0)
            for t in range(NT):
                ts_ = slice(t * 512, (t + 1) * 512)
                a, at, ad = ab[t % 3]
                h, ht, hd = hb[t % 3]
                if t + 1 < NT:
                    load2(t + 1)
                for m in range(8):
                    p, pt = po[it % 4]
                    it += 1
                    for j in range(22):
                        S.op("pe", lambda e, p=p, j=j, m=m, a=a: e.matmul(
                            p[:], lhsT=wo[:, j, m * 128:(m + 1) * 128], rhs=a[:, j, :], start=(j == 0), stop=(j == 21)),
                            reads=[wot[j], at], writes=[pt])
                    S.op("dve", lambda e, p=p, h=h, m=m: e.scalar_tensor_tensor(
                        out=h[:, m, :], in0=p[:], scalar=0.5, in1=h[:, m, :], op0=ALU.mult, op1=ALU.add),
                        reads=[pt, ht], writes=[ht])
                self.dma("sp", self.hT[t], h[:], reads=[ht], dsem=hd)
                if t >= 1:
                    hp_, htp_, _ = hb[(t - 1) % 3]
                    self.norm_epilogue(c, hp_, htp_, t - 1, next_gcol)
            hp_, htp_, _ = hb[(NT - 1) % 3]
            self.norm_epilogue(c, hp_, htp_, NT - 1, next_gcol)

    def rope_tables(self):
        S = self.S
        T = self.T
        with self.phase():
            c = self.common(0, need_cst=True)
            cst, ctrk = c["cst"], c["ctrk"]
            pi_ = self.sb([64, T], I32)
            ang = self.sb([64, T], F32)
            kf = self.sb([64, T], F32)
            ki = self.sb([64, T], I32)
            r = self.sb([64, T], F32)
            m = self.sb([64, T], F32)
            o = self.sb([64, T], F32)
            t = Trk()
            HI = 6.28125
            LO = 2.0 * np.pi - 6.28125
            self.dma("sp", pi_[:], self.pos.partition_broadcast(64)[:, 0, :], writes=[t], dsem=S.dsem())
            S.op("dve", lambda e: e.tensor_copy(out=ang[:], in_=pi_[:]), reads=[t], writes=[t])
            S.op("dve", lambda e: e.tensor_scalar(out=ang[:], in0=ang[:], scalar1=cst[0:64, K_INV:K_INV + 1], scalar2=None, op0=ALU.mult), reads=[t, ctrk], writes=[t])

            def wrap(buf):
                S.op("dve", lambda e: e.tensor_scalar(out=m[:], in0=buf[:], scalar1=PI, scalar2=-2.0 * PI, op0=ALU.is_gt, op1=ALU.mult), reads=[t], writes=[t])
                S.op("dve", lambda e: e.tensor_tensor(out=buf[:], in0=buf[:], in1=m[:], op=ALU.add), reads=[t], writes=[t])
                S.op("dve", lambda e: e.tensor_scalar(out=m[:], in0=buf[:], scalar1=-PI, scalar2=2.0 * PI, op0=ALU.is_lt, op1=ALU.mult), reads=[t], writes=[t])
                S.op("dve", lambda e: e.tensor_tensor(out=buf[:], in0=buf[:], in1=m[:], op=ALU.add), reads=[t], writes=[t])
                S.op("dve", lambda e: e.tensor_scalar(out=buf[:], in0=buf[:], scalar1=PI, scalar2=-PI, op0=ALU.min, op1=ALU.max), reads=[t], writes=[t])
            S.op("dve", lambda e: e.tensor_scalar(out=kf[:], in0=ang[:], scalar1=float(1.0 / (2.0 * np.pi)), scalar2=None, op0=ALU.mult), reads=[t], writes=[t])
            S.op("dve", lambda e: e.tensor_copy(out=ki[:], in_=kf[:]), reads=[t], writes=[t])
            S.op("dve", lambda e: e.tensor_copy(out=kf[:], in_=ki[:]), reads=[t], writes=[t])
            S.op("dve", lambda e: e.scalar_tensor_tensor(out=r[:], in0=kf[:], scalar=-HI, in1=ang[:], op0=ALU.mult, op1=ALU.add), reads=[t], writes=[t])
            S.op("dve", lambda e: e.scalar_tensor_tensor(out=r[:], in0=kf[:], scalar=-LO, in1=r[:], op0=ALU.mult, op1=ALU.add), reads=[t], writes=[t])
            wrap(r)
            S.op("act", lambda e: e.activation(out=o[:], in_=r[:], func=AF.Sin), reads=[t], writes=[t])
            S.op("dve", lambda e: e.tensor_scalar(out=o[:], in0=o[:], scalar1=cst[0:64, K_SGN:K_SGN + 1], scalar2=None, op0=ALU.mult), reads=[t, ctrk], writes=[t])
            self.dma("sp", self.ropeT[1], o[:], reads=[t], dsem=S.dsem())
            S.op("dve", lambda e: e.tensor_scalar(out=r[:], in0=r[:], scalar1=PI / 2.0, scalar2=None, op0=ALU.add), reads=[t], writes=[t])
            wrap(r)
            S.op("act", lambda e: e.activation(out=kf[:], in_=r[:], func=AF.Sin), reads=[t], writes=[t])
            self.dma("sp", self.ropeT[0], kf[:], reads=[t], dsem=S.dsem())

    def inproj(self, l):
        S = self.S
        T, NT, NB = self.T, self.NT, self.NB
        w = self.w["w_in_mix"][l]
        wv = w.rearrange("(kc p) n -> p kc n", p=128)
        with self.phase():
            c = self.common(l)
            pvt, pvtrk = c["pvt"], c["pvtrk"]
            xn = self.sb([128, 8, T], BF16)
            xtrk = [Trk() for _ in range(NT)]
            self.xn_load(xn, xtrk)
            blocks = [(i * 256, 256) for i in range(10)] + [(2576, 256), (2832, 128), (2960, 256), (3216, 64)]
            NWB = 3
            wb = [(self.sb([128, 8, 256], BF16), Trk(), S.dsem(sw=True)) for _ in range(NWB)]
            pp = [(self.ps([128, 512]), Trk()) for _ in range(6)]
            st = [(self.sb([128, T], F32), Trk(), S.dsem()) for _ in range(3)]
            accs = [(self.sb([128, T], F32), Trk()) for _ in range(3)]
            cob = [(self.sb([128, T], BF16), Trk(), S.dsem()) for _ in range(2)]

            def loadw(bi):
                c0, ncol = blocks[bi]
                wt, wtt, wd = wb[bi % NWB]
                self.dma("pool", wt[:, :, 0:ncol], wv[:, :, c0:c0 + ncol], writes=[wtt], dsem=wd)
            loadw(0)
            loadw(1)
            it = 0
            so = 0
            for bi, (c0, ncol) in enumerate(blocks):
                if bi + 2 < len(blocks):
                    loadw(bi + 2)
                wt, wtt, wd = wb[bi % NWB]
                for cc in range((ncol + 127) // 128):
                    M = min(128, ncol - cc * 128)
                    r0 = c0 + cc * 128
                    is_z = r0 < 1024
                    is_xbc = 1024 <= r0 < 2560
                    sbuf, stt, sd = st[so % 3]
                    so += 1
                    if is_xbc:
                        ch = (r0 - 1024) // 128
                        acc, t_acc = accs[ch % 3]
                    for t in range(NT):
                        ts_ = slice(t * 512, (t + 1) * 512)
                        p, pt = pp[it % 6]
                        for k in range(8):
                            S.op("pe", lambda e, p=p, wt=wt, k=k, cc=cc, M=M, ts_=ts_: e.matmul(
                                p[0:M, :], lhsT=wt[:, k, cc * 128:cc * 128 + M], rhs=xn[:, k, ts_], start=(k == 0), stop=(k == 7)),
                                reads=[wtt, xtrk[t]], writes=[pt])
                        if is_z:
                            S.op("act", lambda e, p=p, sbuf=sbuf, M=M, ts_=ts_: e.activation(out=sbuf[0:M, ts_], in_=p[0:M, :], func=AF.Silu), reads=[pt], writes=[stt])
                        elif is_xbc:
                            S.op("act", lambda e, p=p, sbuf=sbuf, ts_=ts_: e.activation(out=sbuf[:, ts_], in_=p[:, :], func=AF.Copy), reads=[pt], writes=[stt])
                            S.op("act", lambda e, p=p, acc=acc, ts_=ts_, ch=ch: e.activation(
                                out=acc[:, ts_], in_=p[:, :], func=AF.Identity, scale=pvt[:, C_CONVW + 3 * 12 + ch:C_CONVW + 3 * 12 + ch + 1],
                                bias=pvt[:, C_CONVB + ch:C_CONVB + ch + 1]), reads=[pt, pvtrk], writes=[t_acc])
                        elif it % 2 == 0:
                            S.op("act", lambda e, p=p, sbuf=sbuf, M=M, ts_=ts_: e.activation(out=sbuf[0:M, ts_], in_=p[0:M, :], func=AF.Copy), reads=[pt], writes=[stt])
                        else:
                            S.op("dve", lambda e, p=p, sbuf=sbuf, M=M, ts_=ts_: e.tensor_copy(out=sbuf[0:M, ts_], in_=p[0:M, :]), reads=[pt], writes=[stt])
                        it += 1
                    if not is_xbc:
                        self.dma("sp", self.zxT[r0:r0 + M, :], sbuf[0:M, :], reads=[stt], dsem=sd)
                    else:
                        o, ot, od = cob[ch % 2]
                        x = sbuf
                        for sft in (1, 2, 3):
                            wcol = C_CONVW + (3 - sft) * 12 + ch
                            S.op("dve", lambda e, x=x, sft=sft, wcol=wcol, acc=acc: e.scalar_tensor_tensor(
                                out=acc[:, sft:T], in0=x[:, 0:T - sft], scalar=pvt[:, wcol:wcol + 1], in1=acc[:, sft:T], op0=ALU.mult, op1=ALU.add),
                                reads=[stt, t_acc, pvtrk], writes=[t_acc])
                        S.op("act", lambda e, o=o, acc=acc: e.activation(out=o[:], in_=acc[:], func=AF.Silu), reads=[t_acc], writes=[ot])
                        self.dma("sp", self.xbcT[ch * 128:(ch + 1) * 128, :], o[:], reads=[ot], dsem=od)
            wdt = self.sb([128, 8, 16], BF16)
            wdtt = Trk()
            self.dma("pool", wdt[:], wv[:, :, 2560:2576], writes=[wdtt], dsem=S.dsem(sw=True))
            pd, pdt = pp[0]
            for tb in range(NB):
                for k in range(8):
                    S.op("pe", lambda e, tb=tb, k=k: e.matmul(
                        pd[:, tb * 16:(tb + 1) * 16], lhsT=xn[:, k, tb * 128:(tb + 1) * 128], rhs=wdt[:, k, :], start=(k == 0), stop=(k == 7)),
                        reads=[wdtt, xtrk[tb // 4]], writes=[pdt])
            dts = self.sb([128, NB, 16], F32)
            dtt = Trk()
            oneb = self.sb([128, 1], F32)
            S.op("pool", lambda e: e.memset(oneb[:], 1.0), writes=[dtt])
            S.op("dve", lambda e: e.tensor_tensor(
                out=dts[:], in0=pd[:, 0:NB * 16].rearrange("p (b h) -> p b h", h=16),
                in1=pvt[:, C_DTB:C_DTB + 16].unsqueeze(1).to_broadcast([128, NB, 16]), op=ALU.add), reads=[pdt, pvtrk, dtt], writes=[dtt])
            S.op("act", lambda e: e.activation(out=dts[:], in_=dts[:], func=AF.Exp), reads=[dtt], writes=[dtt])
            S.op("act", lambda e: e.activation(out=dts[:], in_=dts[:], func=AF.Ln, bias=oneb[:], scale=1.0), reads=[dtt], writes=[dtt])
            self.dma("sp", self.dtr.rearrange("(b p) h -> p b h", p=128), dts[:], reads=[dtt], dsem=S.dsem())

    def conv(self, l):
        S = self.S
        T = self.T
        with self.phase():
            c = self.common(l)
            pvt, pvtrk = c["pvt"], c["pvtrk"]
            xb = [(self.sb([128, T], F32), Trk(), S.dsem()) for _ in range(2)]
            ac = [(self.sb([128, T], F32), Trk()) for _ in range(2)]
            ob = [(self.sb([128, T], BF16), Trk(), S.dsem()) for _ in range(2)]
            for ch in range(12):
                x, xt, xd = xb[ch % 2]
                a, at = ac[ch % 2]
                o, ot, od = ob[ch % 2]
                self.dma("sp", x[:], self.zxT[1024 + ch * 128:1024 + (ch + 1) * 128, :], writes=[xt], dsem=xd)
                S.op("dve", lambda e, a=a, x=x, ch=ch: e.tensor_scalar(
                    out=a[:], in0=x[:], scalar1=pvt[:, C_CONVW + 3 * 12 + ch:C_CONVW + 3 * 12 + ch + 1],
                    scalar2=pvt[:, C_CONVB + ch:C_CONVB + ch + 1], op0=ALU.mult, op1=ALU.add), reads=[xt, pvtrk], writes=[at])
                for sft in (1, 2, 3):
                    wcol = C_CONVW + (3 - sft) * 12 + ch
                    S.op("dve", lambda e, a=a, x=x, sft=sft, wcol=wcol: e.scalar_tensor_tensor(
                        out=a[:, sft:T], in0=x[:, 0:T - sft], scalar=pvt[:, wcol:wcol + 1], in1=a[:, sft:T], op0=ALU.mult, op1=ALU.add),
                        reads=[xt, at, pvtrk], writes=[at])
                S.op("act", lambda e, a=a, o=o: e.activation(out=o[:], in_=a[:], func=AF.Silu), reads=[at], writes=[ot])
                self.dma("sp", self.xbcT[ch * 128:(ch + 1) * 128, :], o[:], reads=[ot], dsem=od)
            for ch in range(8):
                x, xt, xd = xb[ch % 2]
                self.dma("sp", x[:], self.zxT[ch * 128:(ch + 1) * 128, :], writes=[xt], dsem=xd)
                S.op("act", lambda e, x=x: e.activation(out=x[:], in_=x[:], func=AF.Silu), reads=[xt], writes=[xt])
                self.dma("sp", self.zxT[ch * 128:(ch + 1) * 128, :], x[:], reads=[xt], dsem=xd)

    def ssd(self, l):
        S = self.S
        T, NB = self.T, self.NB
        op = S.op
        with self.phase():
            c = self.common(l, need_cst=True)
            pvt, pvtrk, cst, ctrk = c["pvt"], c["pvtrk"], c["cst"], c["ctrk"]
            ones_b, otrk, epsb = c["ones_b"], c["otrk"], c["epsb"]
            K = Trk()
            identb = self.sb([128, 128], BF16)
            ones_f = self.sb([128, 128], F32)
            A_bc = self.sb([128, 16], F32)
            Dmat = self.sb([128, 16, 128], BF16)
            tri = cst[:, K_TRI:K_TRI + 128]
            u1 = cst[:, K_U1:K_U1 + 128]
            op("dve", lambda e: e.tensor_copy(out=identb[:], in_=cst[:, K_ID:K_ID + 128]), reads=[ctrk], writes=[K])
            op("pool", lambda e: e.memset(ones_f[:], 1.0), writes=[K])
            op("act", lambda e: e.activation(out=A_bc[:], in_=pvt[:, C_ALOG:C_ALOG + 16], func=AF.Exp), reads=[pvtrk], writes=[K])
            op("dve", lambda e: e.tensor_scalar(out=A_bc[:], in0=A_bc[:], scalar1=-1.0, scalar2=None, op0=ALU.mult), reads=[K], writes=[K])
            op("dve", lambda e: e.tensor_tensor(
                out=Dmat[:], in0=cst[:, K_ID:K_ID + 128].unsqueeze(1).to_broadcast([128, 16, 128]),
                in1=pvt[:, C_DSK:C_DSK + 16].unsqueeze(2).to_broadcast([128, 16, 128]), op=ALU.mult), reads=[ctrk, pvtrk], writes=[K])
            bank = [self.ps([128, 512]) for _ in range(8)]
            bt = [Trk() for _ in range(8)]
            b0v = bank[0][:, :].bitcast(BF16)
            b1v = bank[1][:, 0:128].bitcast(BF16)
            xsT = [(self.sb([128, 8, 128], BF16), Trk(), S.dsem()) for _ in range(2)]
            bcT = [(self.sb([128, 4, 128], BF16), Trk(), S.dsem()) for _ in range(3)]
            gT = [(self.sb([128, 8, 128], F32), Trk(), S.dsem()) for _ in range(2)]
            dtl = [(self.sb([128, 16], F32), Trk(), S.dsem()) for _ in range(2)]
            yo = [(self.sb([128, 8, 128], BF16), Trk(), S.dsem()) for _ in range(2)]
            xs_tok = [(self.sb([128, 1024], BF16), Trk()) for _ in range(2)]
            B_tok = [(self.sb([128, 256], BF16), Trk()) for _ in range(2)]
            eacd = [(self.sb([128, 32], F32), Trk()) for _ in range(2)]
            xdt = [(self.sb([128, 1024], BF16), Trk()) for _ in range(2)]
            xdtw = [(self.sb([128, 1024], BF16), Trk()) for _ in range(2)]
            MT = [(self.sb([128, 2048], BF16), Trk()) for _ in range(2)]
            dtA = self.sb([128, 16], F32); acs = self.sb([128, 16], F32); dif = self.sb([128, 16], F32)
            dte = self.sb([128, 16], F32); dd = self.sb([128, 16], F32)
            t_sm = Trk()
            R = self.sb([128, 2048], F32); t_R = Trk()
            E = self.sb([128, 2048], F32); t_E = Trk()
            cbm = self.sb([128, 256], F32); t_cbm = Trk()
            tmpy = self.sb([128, 1024], F32); t_tmpy = Trk()
            ytok = self.sb([128, 1024], BF16); t_ytok = Trk()
            H = self.sb([128, 1024], F32); t_H = Trk()
            prevb = self.sb([128, 1024], BF16); t_prev = Trk()
            yg = self.sb([128, 1024], F32); t_yg = Trk()
            sqy = self.sb([128, 1024], BF16); t_sq = Trk()
            rstd = self.sb([128, 256], F32); t_rs = Trk()
            op("pool", lambda e: e.memset(H[:], 0.0), writes=[t_H])
            op("pool", lambda e: e.memset(prevb[:], 0.0), writes=[t_prev])
            xbv = self.xbcT.rearrange("(k p) t -> p k t", p=128)
            zv = self.zxT[0:1024, :].rearrange("(k p) t -> p k t", p=128)
            dtv_ = self.dtr.rearrange("(b p) h -> p b h", p=128)

            def bc_h64(ap16, h0, nh):
                return ap16[:, h0:h0 + nh].unsqueeze(2).to_broadcast([128, nh, 64])

            def stageA(ci):
                G = [[]]
                def grp():
                    G.append([])
                def op(*a_, **k_):
                    G[-1].append(lambda: S.op(*a_, **k_))
                def dma(*a_, **k_):
                    G[-1].append(lambda: self.dma(*a_, **k_))
                cs = slice(ci * 128, (ci + 1) * 128)
                xs_, t_xsT, d1 = xsT[ci % 2]
                bc_, t_bc, d2 = bcT[ci % 3]
                dt_, t_dt, d4 = dtl[ci % 2]
                xst, t_xs = xs_tok[ci % 2]
                Bt, t_B = B_tok[ci % 2]
                ec, t_ec = eacd[ci % 2]
                xd, t_xdt = xdt[ci % 2]
                xw, t_xdtw = xdtw[ci % 2]
                mt, t_MT = MT[ci % 2]
                dma("sp", xs_[:], xbv[:, 0:8, cs], writes=[t_xsT], dsem=d1)
                dma("sp", bc_[:], xbv[:, 8:12, cs], writes=[t_bc], dsem=d2)
                dma("sp", dt_[:], dtv_[:, ci, :], writes=[t_dt], dsem=d4)
                grp()
                for k in range(8):
                    op("pe", lambda e, k=k: e.transpose(out=b0v[:, k * 128:(k + 1) * 128], in_=xs_[:, k, :], identity=identb[:]),
                       reads=[t_xsT, K], writes=[bt[0]])
                op("act", lambda e: e.activation(out=xst[:], in_=b0v, func=AF.Copy), reads=[bt[0]], writes=[t_xs])
                grp()
                for g in range(2):
                    op("pe", lambda e, g=g: e.transpose(out=b1v[:, g * 128:(g + 1) * 128], in_=bc_[:, g, :], identity=identb[:]),
                       reads=[t_bc, K], writes=[bt[1]])
                op("act", lambda e: e.activation(out=Bt[:], in_=b1v, func=AF.Copy), reads=[bt[1]], writes=[t_B])
                grp()
                op("dve", lambda e: e.tensor_tensor(out=dtA[:], in0=dt_[:], in1=A_bc[:], op=ALU.mult), reads=[t_dt, K], writes=[t_sm])
                op("pe", lambda e: e.matmul(bank[1][:, 128:144], lhsT=tri, rhs=dtA[:], start=True, stop=True), reads=[t_sm, ctrk], writes=[bt[1]])
                op("pe", lambda e: e.matmul(bank[1][:, 144:160], lhsT=ones_f[:], rhs=dtA[:], start=True, stop=True), reads=[t_sm, K], writes=[bt[1]])
                op("act", lambda e: e.activation(out=ec[:, 0:16], in_=bank[1][:, 128:144], func=AF.Exp), reads=[bt[1]], writes=[t_ec])
                op("act", lambda e: e.activation(out=acs[:], in_=bank[1][:, 128:144], func=AF.Copy), reads=[bt[1]], writes=[t_sm])
                op("act", lambda e: e.activation(out=ec[:, 16:32], in_=bank[1][:, 144:160], func=AF.Exp), reads=[bt[1]], writes=[t_ec])
                op("dve", lambda e: e.tensor_tensor(out=dif[:], in0=bank[1][:, 144:160], in1=acs[:], op=ALU.subtract), reads=[bt[1], t_sm], writes=[t_sm])
                grp()
                op("act", lambda e: e.activation(out=dte[:], in_=dif[:], func=AF.Exp), reads=[t_sm], writes=[t_sm])
                op("dve", lambda e: e.tensor_tensor(out=dd[:], in0=dt_[:], in1=dte[:], op=ALU.mult), reads=[t_dt, t_sm], writes=[t_sm])
                grp()
                op("pool", lambda e: e.tensor_tensor(
                    out=xd[:, :].rearrange("p (h d) -> p h d", h=16), in0=xst[:, :].rearrange("p (h d) -> p h d", h=16),
                    in1=bc_h64(dt_, 0, 16), op=ALU.mult), reads=[t_xs, t_dt], writes=[t_xdt])
                grp()
                op("pool", lambda e: e.tensor_tensor(
                    out=xw[:, :].rearrange("p (h d) -> p h d", h=16), in0=xst[:, :].rearrange("p (h d) -> p h d", h=16),
                    in1=bc_h64(dd, 0, 16), op=ALU.mult), reads=[t_xs, t_sm], writes=[t_xdtw])
                grp()
                op("dve", lambda e: e.tensor_tensor(
                    out=R[:, :].rearrange("p (h l) -> p h l", h=16), in0=tri.unsqueeze(1).to_broadcast([128, 16, 128]),
                    in1=dtA[:, :].unsqueeze(2).to_broadcast([128, 16, 128]), op=ALU.mult), reads=[ctrk, t_sm], writes=[t_R])
                for q in range(4):
                    bq = 2 + (q % 2)
                    grp()
                    op("pe", lambda e, q=q, bq=bq: e.matmul(bank[bq][:, :], lhsT=u1, rhs=R[:, q * 512:(q + 1) * 512], start=True, stop=True),
                       reads=[t_R, ctrk], writes=[bt[bq]])
                    op("act", lambda e, q=q, bq=bq: e.activation(out=E[:, q * 512:(q + 1) * 512], in_=bank[bq][:, :], func=AF.Exp),
                       reads=[bt[bq]], writes=[t_E])
                grp()
                for g in range(2):
                    op("pe", lambda e, g=g: e.matmul(bank[1][:, 256 + g * 128:256 + (g + 1) * 128], lhsT=bc_[:, g, :], rhs=bc_[:, 2 + g, :], start=True, stop=True),
                       reads=[t_bc], writes=[bt[1]])
                op("dve", lambda e: e.tensor_tensor(
                    out=cbm[:, :].rearrange("p (g l) -> p g l", g=2), in0=bank[1][:, 256:512].rearrange("p (g l) -> p g l", g=2),
                    in1=tri.unsqueeze(1).to_broadcast([128, 2, 128]), op=ALU.mult), reads=[bt[1], ctrk], writes=[t_cbm])
                grp()
                op("dve", lambda e: e.tensor_tensor(
                    out=mt[:, :].rearrange("p (g h l) -> p g h l", g=2, h=8), in0=E[:, :].rearrange("p (g h l) -> p g h l", g=2, h=8),
                    in1=cbm[:, :].rearrange("p (g l) -> p g l", g=2).unsqueeze(2).to_broadcast([128, 2, 8, 128]), op=ALU.mult),
                    reads=[t_E, t_cbm], writes=[t_MT])
                return [g_ for g_ in G if g_]

            def stageB(ci):
                G = [[]]
                def grp():
                    G.append([])
                def op(*a_, **k_):
                    G[-1].append(lambda: S.op(*a_, **k_))
                def dma(*a_, **k_):
                    G[-1].append(lambda: self.dma(*a_, **k_))
                cs = slice(ci * 128, (ci + 1) * 128)
                bc_, t_bc, d2 = bcT[ci % 3]
                g_, t_g, d3 = gT[ci % 2]
                yo_, t_yo, d5 = yo[ci % 2]
                xst, t_xs = xs_tok[ci % 2]
                Bt, t_B = B_tok[ci % 2]
                ec, t_ec = eacd[ci % 2]
                xd, t_xdt = xdt[ci % 2]
                xw, t_xdtw = xdtw[ci % 2]
                mt, t_MT = MT[ci % 2]
                dma("sp", g_[:], zv[:, :, cs], writes=[t_g], dsem=d3)
                grp()
                for h in range(16):
                    if h % 4 == 0:
                        grp()
                    ob_ = bank[4 + h // 8][:, (h % 8) * 64:(h % 8 + 1) * 64]
                    op("pe", lambda e, h=h, ob_=ob_: e.matmul(ob_, lhsT=mt[:, h * 128:(h + 1) * 128], rhs=xd[:, h * 64:(h + 1) * 64], start=True, stop=False),
                       reads=[t_MT, t_xdt], writes=[bt[4 + h // 8]])
                    op("pe", lambda e, h=h, ob_=ob_: e.matmul(ob_, lhsT=Dmat[:, h, :], rhs=xst[:, h * 64:(h + 1) * 64], start=False, stop=True),
                       reads=[K, t_xs], writes=[bt[4 + h // 8]])
                grp()
                for g in range(2):
                    op("pe", lambda e, g=g: e.matmul(bank[6 + g][:, :], lhsT=bc_[:, 2 + g, :], rhs=prevb[:, g * 512:(g + 1) * 512], start=True, stop=True),
                       reads=[t_bc, t_prev], writes=[bt[6 + g]])
                for g in range(2):
                    hs = slice(g * 512, (g + 1) * 512)
                    grp()
                    op("dve", lambda e, g=g, hs=hs: e.tensor_tensor(
                        out=tmpy[:, hs].rearrange("p (h d) -> p h d", h=8), in0=bank[6 + g][:, :].rearrange("p (h d) -> p h d", h=8),
                        in1=bc_h64(ec, g * 8, 8), op=ALU.mult), reads=[bt[6 + g], t_ec], writes=[t_tmpy])
                    op("dve", lambda e, g=g, hs=hs: e.tensor_tensor(out=ytok[:, hs], in0=bank[4 + g][:, :], in1=tmpy[:, hs], op=ALU.add),
                       reads=[bt[4 + g], t_tmpy], writes=[t_ytok])
                for g in range(2):
                    hs = slice(g * 512, (g + 1) * 512)
                    grp()
                    op("pe", lambda e, g=g, hs=hs: e.matmul(bank[6 + g][:, :], lhsT=Bt[:, g * 128:(g + 1) * 128], rhs=xw[:, hs], start=True, stop=True),
                       reads=[t_B, t_xdtw], writes=[bt[6 + g]])
                    op("dve", lambda e, g=g, hs=hs: e.tensor_tensor(
                        out=H[:, hs].rearrange("p (h d) -> p h d", h=8), in0=H[:, hs].rearrange("p (h d) -> p h d", h=8),
                        in1=bc_h64(ec, 16 + g * 8, 8), op=ALU.mult), reads=[t_H, t_ec], writes=[t_H])
                    op("dve", lambda e, g=g, hs=hs: e.tensor_tensor(out=H[:, hs], in0=bank[6 + g][:, :], in1=H[:, hs], op=ALU.add),
                       reads=[bt[6 + g], t_H], writes=[t_H])
                grp()
                op("act", lambda e: e.activation(out=prevb[:], in_=H[:], func=AF.Copy), reads=[t_H], writes=[t_prev])
                grp()
                for k in range(8):
                    op("pe", lambda e, k=k: e.transpose(out=b0v[:, k * 128:(k + 1) * 128], in_=ytok[:, k * 128:(k + 1) * 128], identity=identb[:]),
                       reads=[t_ytok, K], writes=[bt[0]])
                op("dve", lambda e: e.tensor_tensor(out=yg[:], in0=b0v, in1=g_[:, :, :].rearrange("p k t -> p (k t)"), op=ALU.mult),
                   reads=[bt[0], t_g], writes=[t_yg])
                grp()
                op("act", lambda e: e.activation(out=sqy[:], in_=yg[:], func=AF.Square), reads=[t_yg], writes=[t_sq])
                grp()
                for g in range(2):
                    for k4 in range(4):
                        op("pe", lambda e, g=g, k4=k4: e.matmul(bank[1][:, g * 128:(g + 1) * 128], lhsT=ones_b[:], rhs=sqy[:, (4 * g + k4) * 128:(4 * g + k4 + 1) * 128],
                                                               start=(k4 == 0), stop=(k4 == 3)), reads=[t_sq, otrk], writes=[bt[1]])
                op("act", lambda e: e.activation(out=rstd[:], in_=bank[1][:, 0:256], func=AF.Ln, bias=epsb[:], scale=1.0 / 512), reads=[bt[1], otrk], writes=[t_rs])
                grp()
                op("act", lambda e: e.activation(out=rstd[:], in_=rstd[:], func=AF.Exp, scale=-0.5), reads=[t_rs], writes=[t_rs])
                grp()
                op("pool", lambda e: e.tensor_tensor(
                    out=yg[:, :].rearrange("p (k t) -> p k t", k=8), in0=yg[:, :].rearrange("p (k t) -> p k t", k=8),
                    in1=pvt[:, C_SSDN:C_SSDN + 8].unsqueeze(2).to_broadcast([128, 8, 128]), op=ALU.mult), reads=[t_yg, pvtrk, t_sq], writes=[t_yg])
                grp()
                op("dve", lambda e: e.tensor_tensor(
                    out=yo_[:, :, :].rearrange("p (g k) t -> p g k t", g=2), in0=yg[:, :].rearrange("p (g k t) -> p g k t", g=2, k=4),
                    in1=rstd[:, :].rearrange("p (g t) -> p g t", g=2).unsqueeze(2).to_broadcast([128, 2, 4, 128]), op=ALU.mult),
                    reads=[t_yg, t_rs], writes=[t_yo])
                dma("sp", self.yT[ci // 4, :, 0:8, (ci % 4) * 128:(ci % 4 + 1) * 128], yo_[:], reads=[t_yo], dsem=d5)
                return [g_ for g_ in G if g_]

            def run(groups):
                for g_ in groups:
                    for th in g_:
                        th()

            def interleave(ga, gb):
                na, nb = len(ga), len(gb)
                i = j = 0
                while i < na or j < nb:
                    if j >= nb or (i < na and i * nb <= j * na):
                        run([ga[i]])
                        i += 1
                    else:
                        run([gb[j]])
                        j += 1
            run(stageA(0))
            for ci in range(NB):
                gb = stageB(ci)
                if ci + 1 < NB:
                    interleave(stageA(ci + 1), gb)
                else:
                    run(gb)

    def mla(self, l):
        S = self.S
        T, NT, NB = self.T, self.NT, self.NB
        op = S.op
        wq = self.w["w_q_b"][l].rearrange("(kc p) n -> p kc n", p=128)
        wkv = self.w["w_kv_b"][l].rearrange("(kc p) n -> p kc n", p=128)
        with self.phase():
            c = self.common(l)
            pvt, pvtrk, ones_b, otrk, epsb = c["pvt"], c["pvtrk"], c["ones_b"], c["otrk"], c["epsb"]
            bank = [self.ps([128, 512]) for _ in range(8)]
            bt = [Trk() for _ in range(8)]
            cos2 = self.sb([64, T], F32); sin2 = self.sb([64, T], F32); t_tab = Trk()
            self.dma("sp", cos2[:], self.ropeT[0], writes=[t_tab], dsem=S.dsem())
            self.dma("sp", sin2[:], self.ropeT[1], writes=[t_tab], dsem=S.dsem())
            qn = self.sb([128, 3, T], BF16); kvn = self.sb([128, 2, T], BF16); kr = self.sb([128, T], BF16)
            t_qn = [Trk() for _ in range(NT)]; t_kvn = [Trk() for _ in range(NT)]; t_kr = [Trk() for _ in range(NT)]
            t_pad = Trk()
            ld = [(self.sb([128, 5, 512], F32), Trk(), S.dsem()) for _ in range(2)]
            kl = [(self.sb([64, 2, 512], F32), Trk(), S.dsem()) for _ in range(2)]
            sq = self.sb([128, 5, 512], BF16); t_sq = Trk()
            rs = self.sb([128, 2, 512], F32); t_rs = Trk()
            tmp1 = self.sb([64, 512], F32); tmp2 = self.sb([64, 512], F32); t_tmp = Trk()
            for t in range(NT):
                ts_ = slice(t * 512, (t + 1) * 512)
                x_, t_x, d_x = ld[t % 2]
                k_, t_k, d_k = kl[t % 2]
                self.dma("sp", x_[:, 0:3, :], self.zxT[2576:2960, ts_].rearrange("(k p) t -> p k t", p=128), writes=[t_x], dsem=d_x)
                self.dma("sp", x_[:, 3:5, :], self.zxT[2960:3216, ts_].rearrange("(k p) t -> p k t", p=128), writes=[t_x], dsem=d_x)
                self.dma("sp", k_[:, 0, :], self.zxT[3216:3280, ts_], writes=[t_k], dsem=d_k)
                self.dma("sp", k_[0:32, 1, :], self.zxT[3248:3280, ts_], writes=[t_k], dsem=d_k)
                self.dma("sp", k_[32:64, 1, :], self.zxT[3216:3248, ts_], writes=[t_k], dsem=d_k)
                op("act", lambda e, x_=x_: e.activation(out=sq[:], in_=x_[:], func=AF.Square), reads=[t_x], writes=[t_sq])
                for k in range(3):
                    op("pe", lambda e, k=k: e.matmul(bank[7][:, :], lhsT=ones_b[:], rhs=sq[:, k, :], start=(k == 0), stop=(k == 2)), reads=[t_sq, otrk], writes=[bt[7]])
                for k in range(2):
                    op("pe", lambda e, k=k: e.matmul(bank[6][:, :], lhsT=ones_b[:], rhs=sq[:, 3 + k, :], start=(k == 0), stop=(k == 1)), reads=[t_sq, otrk], writes=[bt[6]])
                op("act", lambda e: e.activation(out=rs[:, 0, :], in_=bank[7][:, :], func=AF.Ln, bias=epsb[:], scale=1.0 / 384), reads=[bt[7], otrk], writes=[t_rs])
                op("act", lambda e: e.activation(out=rs[:, 1, :], in_=bank[6][:, :], func=AF.Ln, bias=epsb[:], scale=1.0 / 256), reads=[bt[6], otrk], writes=[t_rs])
                op("act", lambda e: e.activation(out=rs[:], in_=rs[:], func=AF.Exp, scale=-0.5), reads=[t_rs], writes=[t_rs])
                for k in range(3):
                    op("dve", lambda e, k=k, x_=x_, ts_=ts_: e.scalar_tensor_tensor(
                        out=qn[:, k, ts_], in0=x_[:, k, :], scalar=pvt[:, C_QAN + k:C_QAN + k + 1], in1=rs[:, 0, :], op0=ALU.mult, op1=ALU.mult),
                        reads=[t_x, t_rs, pvtrk], writes=[t_qn[t]])
                for k in range(2):
                    op("dve", lambda e, k=k, x_=x_, ts_=ts_: e.scalar_tensor_tensor(
                        out=kvn[:, k, ts_], in0=x_[:, 3 + k, :], scalar=pvt[:, C_KVAN + k:C_KVAN + k + 1], in1=rs[:, 1, :], op0=ALU.mult, op1=ALU.mult),
                        reads=[t_x, t_rs, pvtrk], writes=[t_kvn[t]])
                op("dve", lambda e, k_=k_, ts_=ts_: e.tensor_tensor(out=tmp1[:], in0=k_[:, 0, :], in1=cos2[:, ts_], op=ALU.mult), reads=[t_k, t_tab], writes=[t_tmp])
                op("dve", lambda e, k_=k_, ts_=ts_: e.tensor_tensor(out=tmp2[:], in0=k_[:, 1, :], in1=sin2[:, ts_], op=ALU.mult), reads=[t_k, t_tab], writes=[t_tmp])
                op("dve", lambda e, ts_=ts_: e.tensor_tensor(out=kr[0:64, ts_], in0=tmp1[:], in1=tmp2[:], op=ALU.add), reads=[t_tmp], writes=[t_kr[t]])
            wqh = [(self.sb([128, 3, 256], BF16), [Trk(), Trk(), Trk()], [S.dsem(sw=True), S.dsem(sw=True), S.dsem(sw=True)]) for _ in range(2)]
            wkh = [(self.sb([128, 2, 256], BF16), Trk(), S.dsem(sw=True)) for _ in range(2)]
            qnope = self.sb([128, T], BF16); qr = self.sb([128, T], BF16); knope = self.sb([128, T], BF16); V = self.sb([128, NB, 128], BF16)
            t_qnope = [Trk() for _ in range(NT)]; t_qr = [Trk() for _ in range(NT)]; t_kn = [Trk() for _ in range(NT)]; t_V = [Trk() for _ in range(NT)]
            PT = [(self.sb([128, 512], BF16), Trk()) for _ in range(4)]
            op("pool", lambda e: e.memset(kr[64:128, :], 0.0), writes=[t_pad])
            op("pool", lambda e: e.memset(qr[64:128, :], 0.0), writes=[t_pad])
            SB = [0, 1, 2, 7]
            rl = self.sb([128, 512], F32); t_rl = Trk()
            lacc = [self.sb([128, 512], F32) for _ in range(2)]; t_lacc = [Trk(), Trk()]
            ones_f = self.sb([128, 128], F32); t_onesf = Trk()
            op("pool", lambda e: e.memset(ones_f[:], 1.0), writes=[t_onesf])
            ost = [(self.sb([128, 512], BF16), Trk(), S.dsem()) for _ in range(2)]

            def loadw(h):
                a, ta, da = wqh[h % 2]
                b_, tb_, db = wkh[h % 2]
                q0 = h * 192
                self.dma("pool", a[:, :, 0:192], wq[:, :, q0:q0 + 192], writes=[ta[0]], dsem=da[0])
                self.dma("pool", a[:, :, 192:224], wq[:, :, q0 + 160:q0 + 192], writes=[ta[1]], dsem=da[1])
                self.dma("pool", a[:, :, 224:256], wq[:, :, q0 + 128:q0 + 160], writes=[ta[2]], dsem=da[2])
                self.dma("pool", b_[:], wkv[:, :, h * 256:(h + 1) * 256], writes=[tb_], dsem=db)
            loadw(0)
            si = 0
            oi = 0
            for h in range(8):
                if h + 1 < 8:
                    loadw(h + 1)
                a, ta, da = wqh[h % 2]
                b_, tb_, db = wkh[h % 2]
                for t in range(NT):
                    ts_ = slice(t * 512, (t + 1) * 512)
                    for k in range(3):
                        op("pe", lambda e, k=k, a=a, ts_=ts_: e.matmul(bank[0][:, :], lhsT=a[:, k, 0:128], rhs=qn[:, k, ts_], start=(k == 0), stop=(k == 2)),
                           reads=[ta[0], t_qn[t]], writes=[bt[0]])
                    op("act", lambda e, ts_=ts_: e.activation(out=qnope[:, ts_], in_=bank[0][:, :], func=AF.Copy), reads=[bt[0]], writes=[t_qnope[t]])
                    for k in range(3):
                        op("pe", lambda e, k=k, a=a, ts_=ts_: e.matmul(bank[1][0:64, :], lhsT=a[:, k, 128:192], rhs=qn[:, k, ts_], start=(k == 0), stop=(k == 2)),
                           reads=[ta[0], t_qn[t]], writes=[bt[1]])
                    for k in range(3):
                        op("pe", lambda e, k=k, a=a, ts_=ts_: e.matmul(bank[2][0:64, :], lhsT=a[:, k, 192:256], rhs=qn[:, k, ts_], start=(k == 0), stop=(k == 2)),
                           reads=[ta[1], ta[2], t_qn[t]], writes=[bt[2]])
                    op("dve", lambda e, ts_=ts_: e.tensor_tensor(out=tmp1[:], in0=bank[1][0:64, :], in1=cos2[:, ts_], op=ALU.mult), reads=[bt[1], t_tab], writes=[t_tmp])
                    op("dve", lambda e, ts_=ts_: e.tensor_tensor(out=tmp2[:], in0=bank[2][0:64, :], in1=sin2[:, ts_], op=ALU.mult), reads=[bt[2], t_tab], writes=[t_tmp])
                    op("dve", lambda e, ts_=ts_: e.tensor_tensor(out=qr[0:64, ts_], in0=tmp1[:], in1=tmp2[:], op=ALU.add), reads=[t_tmp], writes=[t_qr[t]])
                    for k in range(2):
                        op("pe", lambda e, k=k, b_=b_, ts_=ts_: e.matmul(bank[0][:, :], lhsT=b_[:, k, 0:128], rhs=kvn[:, k, ts_], start=(k == 0), stop=(k == 1)),
                           reads=[tb_, t_kvn[t]], writes=[bt[0]])
                    op("act", lambda e, ts_=ts_: e.activation(out=knope[:, ts_], in_=bank[0][:, :], func=AF.Copy), reads=[bt[0]], writes=[t_kn[t]])
                    for i in range(4):
                        blk = slice(t * 512 + i * 128, t * 512 + (i + 1) * 128)
                        for k in range(2):
                            op("pe", lambda e, k=k, b_=b_, i=i, blk=blk: e.matmul(bank[7][:, i * 128:(i + 1) * 128], lhsT=kvn[:, k, blk], rhs=b_[:, k, 128:256], start=(k == 0), stop=(k == 1)),
                               reads=[tb_, t_kvn[t]], writes=[bt[7]])
                    op("dve", lambda e, t=t: e.tensor_copy(out=V[:, t * 4:(t + 1) * 4, :].rearrange("p b d -> p (b d)"), in_=bank[7][:, :]), reads=[bt[7]], writes=[t_V[t]])
                for s_ in range(NT):
                    nkb = 4 * s_ + 4
                    ob = 3 if (oi % 2 == 0) else 5
                    slots = {}

                    def emitS(kb, s_=s_):
                        nonlocal si
                        c0 = max(0, kb - 4 * s_) * 128
                        qs = slice(s_ * 512 + c0, (s_ + 1) * 512)
                        ks = slice(kb * 128, (kb + 1) * 128)
                        sl_ = si % 4
                        sb_ = SB[sl_]
                        si += 1
                        slots[kb] = sl_
                        P_, t_P = PT[sl_]
                        op("pe", lambda e: e.matmul(bank[sb_][:, c0:512], lhsT=knope[:, ks], rhs=qnope[:, qs], start=True, stop=False),
                           reads=[t_kn[kb // 4], t_qnope[s_]], writes=[bt[sb_]])
                        op("pe", lambda e: e.matmul(bank[sb_][:, c0:512], lhsT=kr[:, ks], rhs=qr[:, qs], start=False, stop=True),
                           reads=[t_kr[kb // 4], t_qr[s_], t_pad], writes=[bt[sb_]])
                        if kb < 4 * s_:
                            op("act", lambda e: e.activation(out=P_[:], in_=bank[sb_][:, :], func=AF.Exp, scale=SM_SCALE), reads=[bt[sb_]], writes=[t_P])
                        else:
                            op("act", lambda e: e.activation(out=P_[0:64, c0:512], in_=bank[sb_][0:64, c0:512], func=AF.Exp, scale=SM_SCALE), reads=[bt[sb_]], writes=[t_P])
                            op("act", lambda e: e.activation(out=P_[64:128, c0 + 64:512], in_=bank[sb_][64:128, c0 + 64:512], func=AF.Exp, scale=SM_SCALE), reads=[bt[sb_]], writes=[t_P])
                            op("act", lambda e: e.activation(out=P_[64:128, c0:c0 + 64], in_=bank[sb_][64:128, c0:c0 + 64], func=AF.Identity, scale=0.0), reads=[bt[sb_]], writes=[t_P])

                    def emitPV(kb, s_=s_, nkb=nkb, ob=ob):
                        c0 = max(0, kb - 4 * s_) * 128
                        P_, t_P = PT[slots[kb]]
                        op("pe", lambda e: e.matmul(bank[ob][:, c0:512], lhsT=V[:, kb, :], rhs=P_[:, c0:512], start=(kb == 0), stop=(kb == nkb - 1)),
                           reads=[t_V[kb // 4], t_P], writes=[bt[ob]])
                        la_ = lacc[kb % 2]
                        tl_ = t_lacc[kb % 2]
                        if kb < 2:
                            if c0 > 0:
                                op("pool", lambda e: e.memset(la_[:, 0:c0], 0.0), writes=[tl_])
                            op("dve", lambda e: e.tensor_copy(out=la_[:, c0:512], in_=P_[:, c0:512]), reads=[t_P], writes=[tl_])
                        else:
                            op("dve", lambda e: e.tensor_tensor(out=la_[:, c0:512], in0=la_[:, c0:512], in1=P_[:, c0:512], op=ALU.add), reads=[t_P, tl_], writes=[tl_])
                    LA = 3
                    for kb in range(min(LA, nkb)):
                        emitS(kb)
                    for kb in range(nkb):
                        emitPV(kb)
                        if kb + LA < nkb:
                            emitS(kb + LA)
                    o_, t_o, d_o = ost[oi % 2]
                    oi += 1
                    op("pe", lambda e, ob=ob: e.matmul(bank[ob + 1][:, :], lhsT=ones_f[:], rhs=lacc[0][:], start=True, stop=False), reads=[t_onesf, t_lacc[0]], writes=[bt[ob + 1]])
                    op("pe", lambda e, ob=ob: e.matmul(bank[ob + 1][:, :], lhsT=ones_f[:], rhs=lacc[1][:], start=False, stop=True), reads=[t_onesf, t_lacc[1]], writes=[bt[ob + 1]])
                    op("act", lambda e, ob=ob: e.activation(out=rl[:], in_=bank[ob + 1][:, :], func=AF.Ln), reads=[bt[ob + 1]], writes=[t_rl])
                    op("act", lambda e: e.activation(out=rl[:], in_=rl[:], func=AF.Exp, scale=-1.0), reads=[t_rl], writes=[t_rl])
                    op("dve", lambda e, o_=o_, ob=ob: e.tensor_tensor(out=o_[:], in0=bank[ob][:, :], in1=rl[:], op=ALU.mult), reads=[bt[ob], t_rl], writes=[t_o])
                    self.dma("sp", self.yT[s_, :, 8 + h, :], o_[:], reads=[t_o], dsem=d_o)

    def outproj(self, l):
        S = self.S
        NT = self.NT
        w = self.w["w_out_mix"][l]
        with self.phase():
            c = self.common(l)
            self.norm_epilogue_setup(c)
            wo = self.sb([128, 16, D], BF16)
            wot = [Trk() for _ in range(16)]
            for j in range(16):
                self.dma("pool", wo[:, j, :], w[j * 128:(j + 1) * 128, :], writes=[wot[j]], dsem=S.dsem(sw=True))
            ab = [(self.sb([128, 16, 512], BF16), Trk(), S.dsem()) for _ in range(3)]
            hb = [(self.sb([128, 8, 512], F32), Trk(), S.dsem()) for _ in range(3)]
            po = [(self.ps([128, 512]), Trk()) for _ in range(4)]
            it = 0
            def load2(t):
                a, at, ad = ab[t % 3]
                h, ht, hd = hb[t % 3]
                self.dma("sp", a[:], self.yT[t], writes=[at], dsem=ad)
                self.dma("sp", h[:], self.hT[t], writes=[ht], dsem=hd)
            load2(0)
            for t in range(NT):
                ts_ = slice(t * 512, (t + 1) * 512)
                a, at, ad = ab[t % 3]
                h, ht, hd = hb[t % 3]
                if t + 1 < NT:
                    load2(t + 1)
                for m in range(8):
                    p, pt = po[it % 4]
                    it += 1
                    for j in range(16):
                        S.op("pe", lambda e, p=p, j=j, m=m, a=a: e.matmul(
                            p[:], lhsT=wo[:, j, m * 128:(m + 1) * 128], rhs=a[:, j, :], start=(j == 0), stop=(j == 15)),
                            reads=[wot[j], at], writes=[pt])
                    S.op("dve", lambda e, p=p, h=h, m=m: e.tensor_tensor(out=h[:, m, :], in0=p[:], in1=h[:, m, :], op=ALU.add),
                         reads=[pt, ht], writes=[ht])
                self.dma("sp", self.hT[t], h[:], reads=[ht], dsem=hd)
                if t >= 1:
                    hp_, htp_, _ = hb[(t - 1) % 3]
                    self.norm_epilogue(c, hp_, htp_, t - 1, C_FFN2N)
            hp_, htp_, _ = hb[(NT - 1) % 3]
            self.norm_epilogue(c, hp_, htp_, NT - 1, C_FFN2N)

    def ple(self, l):
        S = self.S
        T, NT = self.T, self.NT
        wg = self.w["w_ple_gate"][l].rearrange("(kc p) n -> p kc n", p=128)
        wp = self.w["w_ple_proj"][l].rearrange("(kc p) n -> p kc n", p=128)
        with self.phase():
            c = self.common(l)
            xn = self.sb([128, 8, T], BF16)
            xtrk = [Trk() for _ in range(NT)]
            self.xn_load(xn, xtrk)
            pb = self.sb([128, 2, T], BF16)
            t_pb = Trk()
            self.dma("pool", pb[:], self.pT[l].rearrange("(k p) t -> p k t", p=128), writes=[t_pb], dsem=S.dsem(sw=True))
            wgb = [(self.sb([128, 8, 128], BF16), Trk(), S.dsem(sw=True)) for _ in range(3)]
            wpb = [(self.sb([128, 2, 128], BF16), Trk(), S.dsem(sw=True)) for _ in range(3)]
            hb = [(self.sb([128, T], F32), Trk(), S.dsem()) for _ in range(3)]
            pg = [(self.ps([128, 512]), Trk()) for _ in range(3)]
            pp = [(self.ps([128, 512]), Trk()) for _ in range(3)]
            sg = [(self.sb([128, 512], F32), Trk()) for _ in range(3)]

            def loadw(m):
                a, ta, da = wgb[m % 3]
                b_, tb_, db = wpb[m % 3]
                self.dma("pool", a[:], wg[:, :, m * 128:(m + 1) * 128], writes=[ta], dsem=da)
                self.dma("pool", b_[:], wp[:, :, m * 128:(m + 1) * 128], writes=[tb_], dsem=db)
            def loadh(m):
                h, ht, hd = hb[m % 3]
                self.dma("sp", h[:, :].rearrange("p (t c) -> p t c", c=512), self.hT[:, :, m, :].rearrange("t p c -> p t c"), writes=[ht], dsem=hd)
            loadw(0)
            loadw(1)
            it = 0
            for m in range(8):
                if m + 2 < 8:
                    loadw(m + 2)
                a, ta, da = wgb[m % 3]
                b_, tb_, db = wpb[m % 3]
                h, ht, hd = hb[m % 3]
                if m == 0:
                    loadh(0)
                if m + 1 < 8:
                    loadh(m + 1)
                for t in range(NT):
                    ts_ = slice(t * 512, (t + 1) * 512)
                    p1, p1t = pg[it % 3]
                    p2, p2t = pp[it % 3]
                    s1, s1t = sg[it % 3]
                    it += 1
                    for k in range(8):
                        S.op("pe", lambda e, p1=p1, a=a, k=k, ts_=ts_: e.matmul(p1[:], lhsT=a[:, k, :], rhs=xn[:, k, ts_], start=(k == 0), stop=(k == 7)),
                             reads=[ta, xtrk[t]], writes=[p1t])
                    for k in range(2):
                        S.op("pe", lambda e, p2=p2, b_=b_, k=k, ts_=ts_: e.matmul(p2[:], lhsT=b_[:, k, :], rhs=pb[:, k, ts_], start=(k == 0), stop=(k == 1)),
                             reads=[tb_, t_pb], writes=[p2t])
                    S.op("act", lambda e, s1=s1, p1=p1: e.activation(out=s1[:], in_=p1[:], func=AF.Sigmoid), reads=[p1t], writes=[s1t])
                    S.op("dve", lambda e, s1=s1, p2=p2: e.tensor_tensor(out=s1[:], in0=p2[:], in1=s1[:], op=ALU.mult), reads=[p2t, s1t], writes=[s1t])
                    S.op("dve", lambda e, s1=s1, h=h, ts_=ts_: e.tensor_tensor(out=h[:, ts_], in0=h[:, ts_], in1=s1[:], op=ALU.add), reads=[s1t, ht], writes=[ht])
                self.dma("sp", self.hT[:, :, m, :].rearrange("t p c -> p t c"), h[:, :].rearrange("p (t c) -> p t c", c=512), reads=[ht], dsem=hd)

    def full(self):
        self.rope_tables()
        src = self.xT
        for l in range(NL):
            self.ffn(l, 1, src)
            src = self.hT
            self.inproj(l)
            self.ssd(l)
            self.mla(l)
            self.outproj(l)
            self.ffn(l, 2, self.hT, prenormed=True)
            self.ple(l)
        self.final(self.hT)

    def final(self, hsrc):
        with self.phase():
            c = self.common(0)
            self.norm_phase(hsrc, 0, C_FINN, None, None, c["pvt"], c["pvtrk"], c["ones_b"], c["otrk"], c["epsb"], final_out=self.outT)


def build(T=4096, stages=None):
    nc = bass.Bass("TRN2", target_bir_lowering=False)
    import contextlib
    b = Builder(nc, T)
    with contextlib.ExitStack() as es:
        b.setup(es)
        if stages == "ffn":
            b.ffn(0, 1, b.xT)
            b.final(b.hT)
        else:
            b.full()
    return nc


def pack_pv(inp):
    pv = np.zeros((NL, 128, NPV), np.float32)

    def col(v, n):
        return np.asarray(v, np.float32).reshape(n, 128).T
    for l in range(NL):
        pv[l, :, C_FFN1N:C_FFN1N + 8] = col(inp["ffn1_norm"][l], 8)
        pv[l, :, C_MIXN:C_MIXN + 8] = col(inp["mix_norm"][l], 8)
        pv[l, :, C_FFN2N:C_FFN2N + 8] = col(inp["ffn2_norm"][l], 8)
        pv[l, :, C_PLEN:C_PLEN + 8] = col(inp["ple_norm"][l], 8)
        pv[l, :, C_FINN:C_FINN + 8] = col(inp["final_norm"], 8)
        for w in range(4):
            pv[l, :, C_CONVW + w * 12:C_CONVW + (w + 1) * 12] = col(inp["conv_w"][l, w], 12)
        pv[l, :, C_CONVB:C_CONVB + 12] = col(inp["conv_b"][l], 12)
        pv[l, :, C_SSDN:C_SSDN + 8] = col(inp["ssd_norm"][l], 8)
        pv[l, :, C_QAN:C_QAN + 3] = col(inp["q_a_norm"][l], 3)
        pv[l, :, C_KVAN:C_KVAN + 2] = col(inp["kv_a_norm"][l], 2)
        pv[l, :, C_DTB:C_DTB + 16] = np.asarray(inp["dt_bias"][l], np.float32)[None, :]
        pv[l, :, C_ALOG:C_ALOG + 16] = np.asarray(inp["a_log"][l], np.float32)[None, :]
        pv[l, :, C_DSK:C_DSK + 16] = np.asarray(inp["d_skip"][l], np.float32)[None, :]
    return pv


def make_cst():
    c = np.zeros((128, NCST), np.float32)
    i = np.arange(128)
    c[:, K_ID:K_ID + 128] = np.eye(128, dtype=np.float32)
    c[:, K_TRI:K_TRI + 128] = (i[:, None] <= i[None, :]).astype(np.float32)
    c[:, K_U1:K_U1 + 128] = (i[:, None] > i[None, :]).astype(np.float32)
    inv = (np.float32(10000.0) ** (-np.arange(0, 64, 2, dtype=np.float32) / np.float32(64))).astype(np.float32)
    c[:64, K_INV] = np.concatenate([inv, inv])
    c[:32, K_SGN] = -1.0
    c[32:64, K_SGN] = 1.0
    return c


def to_stream(x):
    x = np.asarray(x, np.float32)
    T = x.shape[0]
    return np.ascontiguousarray(x.reshape(T // 512, 512, 8, 128).transpose(0, 3, 2, 1))


def from_stream(y):
    nt = y.shape[0]
    return np.ascontiguousarray(np.asarray(y).transpose(0, 3, 2, 1).reshape(nt * 512, 1024))


def core_inputs(inp, bidx):
    m = {
        "xT": to_stream(inp["x"][bidx]),
        "pT": np.ascontiguousarray(np.transpose(np.asarray(inp["p"][:, bidx], np.float32), (0, 2, 1))),
        "pos": np.ascontiguousarray(np.asarray(inp["positions"][bidx], np.int32)[None, :]),
        "pv": pack_pv(inp),
        "cst": make_cst(),
    }
    for nm in ("ffn1_w_in", "ffn1_w_out", "w_in_mix", "w_q_b", "w_kv_b", "w_out_mix",
               "ffn2_w_in", "ffn2_w_out", "w_ple_gate", "w_ple_proj"):
        m[nm] = np.ascontiguousarray(np.asarray(inp[nm], np.float32))
    return m


_NC_CACHE = {}


def kernel(**inputs):
    T = inputs["x"].shape[1]
    nb = inputs["x"].shape[0]
    if T not in _NC_CACHE:
        _NC_CACHE[T] = build(T)
    nc = _NC_CACHE[T]
    shared = core_inputs(inputs, 0)
    in_maps = []
    for b in range(nb):
        m = dict(shared)
        m["xT"] = to_stream(inputs["x"][b])
        m["pT"] = np.ascontiguousarray(np.transpose(np.asarray(inputs["p"][:, b], np.float32), (0, 2, 1)))
        m["pos"] = np.ascontiguousarray(np.asarray(inputs["positions"][b], np.int32)[None, :])
        in_maps.append(m)
    res = run_bass_kernel_spmd(nc, in_maps, core_ids=list(range(nb)))
    out = np.stack([from_stream(r["outT"]) for r in res.results], axis=0)
    return out.astype(np.float32)
```
